# Optimizing a Trainium2 kernel written in Bass

```python
import jax
import jax.numpy as jnp
from jax import lax
import numpy as np

D_MODEL = 1024
BATCH = 8
SEQ = 2048
DEPTH = 2

GRID_W = 64
CTX_LEN = 256
N_BRANCH = 4
BRANCH_W = D_MODEL // 2
HEAD_DIM = 64
A_HEADS = BRANCH_W // HEAD_DIM
A_KV = 2
W_HEADS = BRANCH_W // HEAD_DIM
W_KV = 2
WINDOW = 128
Q_BLOCK = 128
B_GROUPS = 4
B_GROUP_W = BRANCH_W // B_GROUPS
CHUNK = 128
M_HEADS = 4
M_HD = BRANCH_W // M_HEADS
M_CHUNK = 128
CONV_W = 3
ROPE_THETA = 10000.0
EPS = 1e-6

IN_LAYOUT = (
    ('a_q', A_HEADS * HEAD_DIM), ('a_k', A_KV * HEAD_DIM), ('a_v', A_KV * HEAD_DIM), ('a_gate', BRANCH_W),
    ('w_q', W_HEADS * HEAD_DIM), ('w_k', W_KV * HEAD_DIM), ('w_v', W_KV * HEAD_DIM), ('w_gate', BRANCH_W),
    ('g_u', BRANCH_W), ('g_v', BRANCH_W), ('g_gate', BRANCH_W),
    ('m_qkv', 3 * BRANCH_W), ('m_i', 2 * M_HEADS), ('m_f', 2 * M_HEADS), ('m_o', BRANCH_W), ('m_gate', BRANCH_W),
    ('merge', N_BRANCH * D_MODEL),
)
D_IN = sum(size for _, size in IN_LAYOUT)

kernel_name = 'hybrid_prefix_dit_block'


def rms_norm(x, g):
    xf = x.astype(jnp.float32)
    y = xf * lax.rsqrt(jnp.mean(xf * xf, axis=-1, keepdims=True) + EPS)
    return (y * g.astype(jnp.float32)).astype(x.dtype)


def layer_norm(x, g, b):
    xf = x.astype(jnp.float32)
    mu = jnp.mean(xf, axis=-1, keepdims=True)
    xc = xf - mu
    y = xc * lax.rsqrt(jnp.mean(xc * xc, axis=-1, keepdims=True) + EPS)
    return (y * g.astype(jnp.float32) + b.astype(jnp.float32)).astype(x.dtype)


def split_cols(p):
    out = {}
    start = 0
    for name, size in IN_LAYOUT:
        out[name] = p[..., start:start + size]
        start += size
    return out


def split_heads(t, h):
    return t.reshape(t.shape[0], t.shape[1], h, HEAD_DIM)


def axial_rope(rows):
    row = jnp.repeat(jnp.arange(rows), GRID_W).astype(jnp.float32)
    col = jnp.tile(jnp.arange(GRID_W), rows).astype(jnp.float32)
    n_freq = HEAD_DIM // 4
    inv = ROPE_THETA ** (-jnp.arange(n_freq, dtype=jnp.float32) / n_freq)
    ang = jnp.concatenate([row[:, None] * inv, col[:, None] * inv], axis=-1)
    return jnp.cos(ang), jnp.sin(ang)


def apply_rope(x, cos, sin):
    x1, x2 = jnp.split(x.astype(jnp.float32), 2, axis=-1)
    c = cos[None, :, None, :]
    s = sin[None, :, None, :]
    return jnp.concatenate([x1 * c - x2 * s, x1 * s + x2 * c], axis=-1).astype(x.dtype)


def attend_dense(q, k, v):
    s = jnp.einsum('bqhgd,bkhd->bhgqk', q, k).astype(jnp.float32) * HEAD_DIM ** -0.5
    p = jax.nn.softmax(s, axis=-1).astype(v.dtype)
    return jnp.einsum('bhgqk,bkhd->bqhgd', p, v)


def global_attention(q_c, k_c, v_c, q_x, k_x, v_x, q_norm, k_norm, cos, sin, need_ctx):
    B, n, _ = q_x.shape
    G = A_HEADS // A_KV
    qx = apply_rope(rms_norm(split_heads(q_x, A_HEADS), q_norm), cos, sin)
    kx = apply_rope(rms_norm(split_heads(k_x, A_KV), k_norm), cos, sin)
    kc = rms_norm(split_heads(k_c, A_KV), k_norm)
    vc = split_heads(v_c, A_KV)
    k_all = jnp.concatenate([kc, kx], axis=1)
    v_all = jnp.concatenate([vc, split_heads(v_x, A_KV)], axis=1)
    qb = qx.reshape(B, n // Q_BLOCK, Q_BLOCK, A_KV, G, HEAD_DIM).swapaxes(0, 1)
    yx = lax.map(lambda blk: attend_dense(blk, k_all, v_all), qb)
    yx = yx.swapaxes(0, 1).reshape(B, n, A_HEADS * HEAD_DIM)
    yc = None
    if need_ctx:
        qc = rms_norm(split_heads(q_c, A_HEADS), q_norm).reshape(B, -1, A_KV, G, HEAD_DIM)
        yc = attend_dense(qc, kc, vc).reshape(B, -1, A_HEADS * HEAD_DIM)
    return yc, yx


def window_attention(q_c, k_c, v_c, q_x, k_x, v_x, sink, cos, sin, need_ctx):
    B, n, _ = q_x.shape
    G = W_HEADS // W_KV
    nb = n // Q_BLOCK
    scale = HEAD_DIM ** -0.5
    qx = apply_rope(split_heads(q_x, W_HEADS), cos, sin).reshape(B, nb, Q_BLOCK, W_KV, G, HEAD_DIM)
    kx = apply_rope(split_heads(k_x, W_KV), cos, sin)
    vx = split_heads(v_x, W_KV)
    kc = split_heads(k_c, W_KV)
    vc = split_heads(v_c, W_KV)
    n_ctx = kc.shape[1]
    sink_l = sink.astype(jnp.float32).reshape(W_KV, G)[None, :, :, None, None]

    def band(t):
        tp = jnp.pad(t, ((0, 0), (Q_BLOCK, Q_BLOCK), (0, 0), (0, 0)))
        tp = tp.reshape(B, nb + 2, Q_BLOCK, W_KV, HEAD_DIM)
        return jnp.concatenate([tp[:, :-2], tp[:, 1:-1], tp[:, 2:]], axis=2)

    blk = jnp.arange(nb)[:, None, None]
    q_pos = blk * Q_BLOCK + jnp.arange(Q_BLOCK)[None, :, None]
    k_pos = (blk - 1) * Q_BLOCK + jnp.arange(3 * Q_BLOCK)[None, None, :]
    valid = (jnp.abs(k_pos - q_pos) <= WINDOW) & (k_pos >= 0) & (k_pos < n)

    def sink_softmax(s):
        sk = jnp.broadcast_to(sink_l, s.shape[:-1] + (1,))
        p = jax.nn.softmax(jnp.concatenate([s, sk], axis=-1), axis=-1)
        return p[..., :-1]

    def block(args):
        qb, kb, vb, ok = args
        s_ctx = jnp.einsum('bqhgd,bkhd->bhgqk', qb, kc).astype(jnp.float32) * scale
        s_loc = jnp.einsum('bqhgd,bkhd->bhgqk', qb, kb).astype(jnp.float32) * scale
        s_loc = jnp.where(ok, s_loc, -jnp.inf)
        p = sink_softmax(jnp.concatenate([s_ctx, s_loc], axis=-1)).astype(vb.dtype)
        return (jnp.einsum('bhgqk,bkhd->bqhgd', p[..., :n_ctx], vc)
                + jnp.einsum('bhgqk,bkhd->bqhgd', p[..., n_ctx:], vb))

    yx = lax.map(block, (qx.swapaxes(0, 1), band(kx).swapaxes(0, 1), band(vx).swapaxes(0, 1), valid))
    yx = yx.swapaxes(0, 1).reshape(B, n, W_HEADS * HEAD_DIM)
    yc = None
    if need_ctx:
        qc = split_heads(q_c, W_HEADS).reshape(B, n_ctx, W_KV, G, HEAD_DIM)
        s = jnp.einsum('bqhgd,bkhd->bhgqk', qc, kc).astype(jnp.float32) * scale
        p = sink_softmax(s).astype(vc.dtype)
        yc = jnp.einsum('bhgqk,bkhd->bqhgd', p, vc).reshape(B, n_ctx, W_HEADS * HEAD_DIM)
    return yc, yx


def chunk_spatial_gating(u, v, ln_g, ln_b, w_s, b_s):
    B, n, _ = u.shape
    u = jax.nn.gelu(u)
    v = layer_norm(jax.nn.gelu(v), ln_g, ln_b)
    vc = v.reshape(B, n // CHUNK, CHUNK, B_GROUPS, B_GROUP_W)
    mixed = jnp.einsum('gts,bnsgc->bntgc', w_s, vc) + b_s.T[:, :, None]
    return u * mixed.reshape(B, n, BRANCH_W)


def centred_dwconv(x, w):
    pad = CONV_W // 2
    return lax.conv_general_dilated(x, w[:, None, :], window_strides=(1,), padding=[(pad, pad)],
                                    dimension_numbers=('NWC', 'WIO', 'NWC'),
                                    feature_group_count=x.shape[-1])


def mlstm_scan(q, k, v, log_i, log_f, state):
    B, n, H, d = q.shape
    nc = n // M_CHUNK
    causal = jnp.tril(jnp.ones((M_CHUNK, M_CHUNK), dtype=bool))

    def to_chunks(t):
        return t.reshape((B, nc, M_CHUNK) + t.shape[2:]).swapaxes(0, 1)

    def step(carry, xs):
        C, nv, m = carry
        qc, kc, vc, ic, fc = xs
        b = jnp.cumsum(fc, axis=1).transpose(0, 2, 1)
        ic = ic.transpose(0, 2, 1)
        Dm = b[:, :, :, None] - b[:, :, None, :] + ic[:, :, None, :]
        Dm = jnp.where(causal, Dm, -jnp.inf)
        m_inter = b + m[:, :, None]
        m_t = jnp.maximum(m_inter, jnp.max(Dm, axis=-1))
        w_intra = jnp.exp(Dm - m_t[..., None])
        w_inter = jnp.exp(m_inter - m_t)
        s = jnp.einsum('bthd,bshd->bhts', qc, kc) * w_intra
        num = (jnp.einsum('bhts,bshe->bthe', s, vc)
               + w_inter.transpose(0, 2, 1)[..., None] * jnp.einsum('bhed,bthd->bthe', C, qc))
        den = jnp.sum(s, axis=-1) + w_inter * jnp.einsum('bhd,bthd->bht', nv, qc)
        den = jnp.maximum(jnp.abs(den), jnp.exp(-m_t))
        h = num / den.transpose(0, 2, 1)[..., None]
        bL = b[:, :, -1]
        g = bL[:, :, None] - b + ic
        m_new = jnp.maximum(bL + m, jnp.max(g, axis=-1))
        a = jnp.exp(bL + m - m_new)
        wk = jnp.exp(g - m_new[..., None])
        C_new = a[..., None, None] * C + jnp.einsum('bhs,bshe,bshd->bhed', wk, vc, kc)
        n_new = a[..., None] * nv + jnp.einsum('bhs,bshd->bhd', wk, kc)
        return (C_new, n_new, m_new), h

    carry, hs = lax.scan(step, state, (to_chunks(q), to_chunks(k), to_chunks(v), to_chunks(log_i), to_chunks(log_f)))
    return hs.swapaxes(0, 1).reshape(B, n, H, d), carry


def mlstm_mixer(pc, px, conv_w, b_i, b_f, m_norm, need_ctx):
    def prep(p):
        B, n, _ = p['m_qkv'].shape
        z = jax.nn.silu(centred_dwconv(p['m_qkv'], conv_w)).astype(jnp.float32)
        z = z.reshape(B, n, 3, M_HEADS, M_HD)
        q = z[:, :, 0] * M_HD ** -0.5
        log_i = p['m_i'].astype(jnp.float32).reshape(B, n, 2, M_HEADS) + b_i.astype(jnp.float32)
        log_f = jax.nn.log_sigmoid(p['m_f'].astype(jnp.float32).reshape(B, n, 2, M_HEADS) + b_f.astype(jnp.float32))
        return q, z[:, :, 1], z[:, :, 2], log_i, log_f

    def run(seq, direction, state, reverse):
        q, k, v, li, lf = seq
        xs = (q, k, v, li[:, :, direction], lf[:, :, direction])
        if reverse:
            xs = tuple(jnp.flip(t, axis=1) for t in xs)
        h, st = mlstm_scan(xs[0], xs[1], xs[2], xs[3], xs[4], state)
        return (jnp.flip(h, axis=1) if reverse else h), st

    sc = prep(pc)
    sx = prep(px)
    B = px['m_qkv'].shape[0]
    zero = (jnp.zeros((B, M_HEADS, M_HD, M_HD), jnp.float32),
            jnp.zeros((B, M_HEADS, M_HD), jnp.float32),
            jnp.zeros((B, M_HEADS), jnp.float32))
    hc_f, st_f = run(sc, 0, zero, False)
    hx_f, _ = run(sx, 0, st_f, False)
    hc_b, st_b = run(sc, 1, zero, True)
    hx_b, _ = run(sx, 1, st_b, True)

    def finish(h, o):
        hn = rms_norm(h, m_norm.reshape(M_HEADS, M_HD)).reshape(o.shape)
        return (jax.nn.sigmoid(o.astype(jnp.float32)) * hn).astype(o.dtype)

    y_x = finish(hx_f + hx_b, px['m_o'])
    y_c = finish(hc_f + hc_b, pc['m_o']) if need_ctx else None
    return y_c, y_x


def merge_branches(ys, p, w_br, w_o):
    gates = (p['a_gate'], p['w_gate'], p['g_gate'], p['m_gate'])
    y = jnp.stack([yk * jax.nn.silu(gk) for yk, gk in zip(ys, gates)], axis=2)
    proj = jnp.einsum('bnkw,kwd->bnkd', y, w_br)
    m = p['merge']
    g = jax.nn.sigmoid(m.reshape(m.shape[0], m.shape[1], N_BRANCH, D_MODEL))
    return jnp.sum(g * proj, axis=2) @ w_o


def mixer_sublayer(pc, px, cos, sin, q_norm, k_norm, sink, ln_g, ln_b, w_s, b_s,
                   conv_w, b_i, b_f, m_norm, w_br, w_o, need_ctx):
    ya_c, ya_x = global_attention(pc['a_q'], pc['a_k'], pc['a_v'], px['a_q'], px['a_k'], px['a_v'],
                                  q_norm, k_norm, cos, sin, need_ctx)
    yw_c, yw_x = window_attention(pc['w_q'], pc['w_k'], pc['w_v'], px['w_q'], px['w_k'], px['w_v'],
                                  sink, cos, sin, need_ctx)
    yg_x = chunk_spatial_gating(px['g_u'], px['g_v'], ln_g, ln_b, w_s, b_s)
    ym_c, ym_x = mlstm_mixer(pc, px, conv_w, b_i, b_f, m_norm, need_ctx)
    y_x = merge_branches((ya_x, yw_x, yg_x, ym_x), px, w_br, w_o)
    y_c = None
    if need_ctx:
        yg_c = chunk_spatial_gating(pc['g_u'], pc['g_v'], ln_g, ln_b, w_s, b_s)
        y_c = merge_branches((ya_c, yw_c, yg_c, ym_c), pc, w_br, w_o)
    return y_c, y_x


def setup_inputs(seed: int = 0) -> dict:
    key = jax.random.key(seed)
    ks = jax.random.split(key, 24)
    f32 = jnp.float32

    def nrm(k, shape, s):
        return jax.random.normal(k, shape, f32) * s

    return {
        'x': nrm(ks[0], (BATCH, SEQ, D_MODEL), 1.0),
        'c': nrm(ks[1], (BATCH, D_MODEL), 1.0),
        'ctx': nrm(ks[2], (BATCH, CTX_LEN, D_MODEL), 1.0),
        'c_ctx': nrm(ks[3], (D_MODEL,), 1.0),
        'w_mod': nrm(ks[4], (DEPTH, D_MODEL, 3 * D_MODEL), 0.2 * D_MODEL ** -0.5),
        'b_mod': nrm(ks[5], (DEPTH, 3 * D_MODEL), 0.02),
        'g_pre': 1.0 + nrm(ks[6], (DEPTH, D_MODEL), 0.02),
        'g_post': 1.0 + nrm(ks[7], (DEPTH, D_MODEL), 0.02),
        'w_in': nrm(ks[8], (DEPTH, D_MODEL, D_IN), D_MODEL ** -0.5),
        'a_q_norm': 1.0 + nrm(ks[9], (DEPTH, HEAD_DIM), 0.02),
        'a_k_norm': 1.0 + nrm(ks[10], (DEPTH, HEAD_DIM), 0.02),
        'w_sink': nrm(ks[11], (DEPTH, W_HEADS), 0.5),
        'sg_ln_g': 1.0 + nrm(ks[12], (DEPTH, BRANCH_W), 0.02),
        'sg_ln_b': nrm(ks[13], (DEPTH, BRANCH_W), 0.02),
        'sg_w': nrm(ks[14], (DEPTH, B_GROUPS, CHUNK, CHUNK), CHUNK ** -0.5),
        'sg_b': 1.0 + nrm(ks[15], (DEPTH, B_GROUPS, CHUNK), 0.1),
        'm_conv': nrm(ks[16], (DEPTH, CONV_W, 3 * BRANCH_W), CONV_W ** -0.5),
        'm_b_i': nrm(ks[17], (DEPTH, 2, M_HEADS), 0.1),
        'm_b_f': jnp.linspace(3.0, 6.0, M_HEADS, dtype=f32) + nrm(ks[18], (DEPTH, 2, M_HEADS), 0.1),
        'm_norm': 1.0 + nrm(ks[19], (DEPTH, BRANCH_W), 0.02),
        'w_branch': nrm(ks[20], (DEPTH, N_BRANCH, BRANCH_W, D_MODEL), BRANCH_W ** -0.5),
        'w_out': nrm(ks[21], (DEPTH, D_MODEL, D_MODEL), D_MODEL ** -0.5),
    }


def reference(x, c, ctx, c_ctx, w_mod, b_mod, g_pre, g_post, w_in, a_q_norm, a_k_norm, w_sink,
              sg_ln_g, sg_ln_b, sg_w, sg_b, m_conv, m_b_i, m_b_f, m_norm, w_branch, w_out):
    rows = x.shape[1] // GRID_W
    cos, sin = axial_rope(rows)
    h_ctx = ctx
    for l in range(DEPTH):
        need_ctx = l < DEPTH - 1
        mod_x = jax.nn.silu(c) @ w_mod[l] + b_mod[l]
        mod_c = jax.nn.silu(c_ctx) @ w_mod[l] + b_mod[l]
        shift_x, scale_x, gate_x = jnp.split(mod_x[:, None, :], 3, axis=-1)
        shift_c, scale_c, gate_c = jnp.split(mod_c, 3, axis=-1)
        in_x = rms_norm(x, g_pre[l]) * (1.0 + scale_x) + shift_x
        in_c = rms_norm(h_ctx, g_pre[l]) * (1.0 + scale_c) + shift_c
        y_c, y_x = mixer_sublayer(split_cols(in_c @ w_in[l]), split_cols(in_x @ w_in[l]), cos, sin,
                                  a_q_norm[l], a_k_norm[l], w_sink[l], sg_ln_g[l], sg_ln_b[l], sg_w[l], sg_b[l],
                                  m_conv[l], m_b_i[l], m_b_f[l], m_norm[l], w_branch[l], w_out[l], need_ctx)
        x = x + gate_x * rms_norm(y_x, g_post[l])
        if need_ctx:
            h_ctx = h_ctx + gate_c * rms_norm(y_c, g_post[l])
    return x
```

```python
import contextlib
import numpy as np
import ml_dtypes
import concourse.bass as bass
import concourse.mybir as mybir
from concourse.bass_utils import run_bass_kernel_spmd

F32 = mybir.dt.float32
BF16 = mybir.dt.bfloat16
AF = mybir.ActivationFunctionType
ALU = mybir.AluOpType
AX = mybir.AxisListType

COMPUTE = ("pe", "act", "dve", "pool")
ENGMAP = {"pe": "tensor", "act": "scalar", "dve": "vector", "pool": "gpsimd", "sp": "sync"}


class Buf:
    __slots__ = ("name", "excl")

    def __init__(self, name, excl=False):
        self.name = name
        self.excl = excl

    def __repr__(self):
        return "Buf(%s)" % self.name


class Sched:
    def __init__(self, nc, sems):
        self.nc = nc
        self.sems = sems
        self.ops = []
        self.cost = []
        self.window = 2048
        self.strict = STRICT_SYNC
        self.done = 0
        self.last_w = {}
        self.readers = {}
        self.val = []
        self.cnt = {e: 0 for e in COMPUTE}
        self.dma_rr = {}
        self.dma_cnt = {}
        self.known = {}
        self.finals = {}
        self.nwait = 0

    def add(self, eng, fn, reads=(), writes=(), dma=False, cost=0.3):
        self.ops.append((eng, fn, tuple(reads), tuple(writes), dma))
        self.cost.append(cost)

    def pe(self, fn, r=(), w=(), cost=0.25):
        self.add("pe", fn, r, w, cost=cost)

    def act(self, fn, r=(), w=(), cost=0.6):
        self.add("act", fn, r, w, cost=cost)

    def dve(self, fn, r=(), w=(), cost=0.3):
        self.add("dve", fn, r, w, cost=cost)

    def dma(self, eng, fn, r=(), w=(), cost=4.0):
        self.add(eng, fn, r, w, dma=True, cost=cost)

    def list_schedule(self, lo, n, alld, streams, window):
        ops, cost = self.ops, self.cost
        LAT = 0.7
        dependents = {}
        nun = {}
        ready = {}
        for i in range(lo, n):
            nun[i] = len(alld[i])
            ready[i] = 0.0
            for j in alld[i]:
                dependents.setdefault(j, []).append(i)
        pend = {e: list(v) for e, v in streams.items()}
        free = {e: 0.0 for e in streams}
        order = {e: [] for e in streams}
        cache = {}
        remaining = n - lo
        while remaining:
            best = None
            for e, pl in pend.items():
                if not pl:
                    continue
                c = cache.get(e)
                if c is None:
                    fe = free[e]
                    c = False
                    for pos in range(min(window, len(pl))):
                        i = pl[pos]
                        if nun[i]:
                            continue
                        st = ready[i] if ready[i] > fe else fe
                        if c is False or st < c[0]:
                            c = (st, i, pos)
                            if st <= fe:
                                break
                    cache[e] = c
                if c is not False and (best is None or (c[0], c[1]) < (best[1][0], best[1][1])):
                    best = (e, c)
            assert best is not None, "list scheduler deadlock"
            e, (st, i, pos) = best
            del pend[e][pos]
            order[e].append(i)
            dma = ops[i][4]
            fin = st + cost[i]
            free[e] = st + (0.06 if dma else cost[i])
            cache[e] = None
            for k in dependents.get(i, ()):
                nun[k] -= 1
                t = fin + (LAT if (ops[k][0] != e or dma) else 0.15)
                if t > ready[k]:
                    ready[k] = t
                cache[ops[k][0]] = None
            remaining -= 1
        self.sim_time = max(free.values())
        return order

    def emit(self, barrier=True):
        ops = self.ops
        lo, n = self.done, len(self.ops)
        if lo == n:
            return
        last_w, readers = self.last_w, self.readers
        deps = {}
        alld = {}
        for i in range(lo, n):
            eng, fn, reads, writes, dma = ops[i]
            d = set()
            exw = list(writes) + [r for r in reads if r.excl]
            for r in reads:
                if r.excl:
                    continue
                lw = last_w.get(r)
                if lw is not None:
                    d.add(lw)
            for w in exw:
                lw = last_w.get(w)
                if lw is not None:
                    d.add(lw)
                for rr in readers.get(w, ()):
                    d.add(rr)
            for r in reads:
                if not r.excl:
                    readers.setdefault(r, []).append(i)
            for w in exw:
                last_w[w] = i
                readers[w] = []
            d.discard(i)
            alld[i] = set(j for j in d if j >= lo)
            dd = set()
            for j in d:
                if j < lo and barrier:
                    continue
                ej, _, rj, wj, dmaj = ops[j]
                if (not dma) and (not dmaj) and ej == eng:
                    if eng == "pe":
                        continue
                    if not self.strict:
                        wset = set(wj) | set(r for r in rj if r.excl)
                        if not (wset & set(reads)):
                            continue
                dd.add(j)
            deps[i] = dd
        signal = set()
        for i in range(lo, n):
            signal |= deps[i]
        streams = {}
        for i in range(lo, n):
            streams.setdefault(ops[i][0], []).append(i)
        if self.window > 1:
            streams = self.list_schedule(lo, n, alld, streams, self.window)
        for eng, idxs in streams.items():
            for i in reversed(idxs):
                if not ops[i][4]:
                    signal.add(i)
                    break
        self.val.extend([None] * (n - lo))
        val = self.val
        dma_prev = {}
        issue = []
        for eng, idxs in streams.items():
            issue.extend(idxs)
        for i in issue:
            eng, fn, reads, writes, dma = ops[i]
            if dma:
                pool = self.sems["dma"][eng]
                k = self.dma_rr.get(eng, 0)
                self.dma_rr[eng] = (k + 1) % len(pool)
                s = pool[k]
                c = self.dma_cnt.get((eng, k), 0)
                if c > 0:
                    dma_prev[i] = (s, 16 * c)
                self.dma_cnt[(eng, k)] = c + 1
                val[i] = (s, 16 * (c + 1))
            elif i in signal:
                self.cnt[eng] += 1
                val[i] = (self.sems[eng], self.cnt[eng])
        prev_finals = dict(self.finals)

        def make_stream(eng, idxs):
            def body(e):
                known = self.known.setdefault(eng, {})

                def wait(s, v):
                    if known.get(s, 0) < v:
                        e.wait_ge(s, v)
                        known[s] = v
                        self.nwait += 1
                if barrier:
                    for s, v in prev_finals.items():
                        wait(s, v)
                for i in idxs:
                    _, fn, reads, writes, dma = ops[i]
                    need = {}
                    for j in deps[i]:
                        s, v = val[j]
                        if need.get(s, 0) < v:
                            need[s] = v
                    if i in dma_prev:
                        s, v = dma_prev[i]
                        if need.get(s, 0) < v:
                            need[s] = v
                    for s, v in need.items():
                        wait(s, v)
                    ins = fn(e)
                    if val[i] is not None:
                        ins.then_inc(val[i][0], 16 if dma else 1)
            return body

        with self.nc.Block() as block:
            for eng, idxs in streams.items():
                getattr(block, ENGMAP[eng])(make_stream(eng, idxs))
        for i in range(lo, n):
            if val[i] is not None:
                s, v = val[i]
                if self.finals.get(s, 0) < v:
                    self.finals[s] = v
        self.done = n

    def finish(self):
        finals = dict(self.finals)

        def body(e):
            for s, v in finals.items():
                e.wait_ge(s, v)
        with self.nc.Block() as block:
            block.sync(body)


STRICT_SYNC = True
D = 1024
NCH = 18
NTOK = NCH * 128
TW = 2308
D_IN = 10768
EPS = 1e-6
R_QN, R_KN, R_SINK, R_LNG, R_LNB, R_MN, R_BI, R_BF, R_CONV = 0, 64, 128, 136, 648, 1160, 1672, 1680, 1688
NROW = 1688


def col(ci):
    return 128 * ci + (1 if ci < 2 else 3)


class _Stop(Exception):
    pass


def build(depth=2, taps=(), stop=None):
    nc = bass.Bass("TRN2", target_bir_lowering=False)

    def din(name, shape, dt=F32):
        return nc.dram_tensor(name, list(shape), dt, kind="ExternalInput").ap()
    xin = din("xin", [NTOK, D])
    cc_d = din("cc", [128, 8, 2])
    wmod_d = din("w_mod", [2, D, 3 * D])
    bmodT_d = din("b_modT", [2, 128, 24])
    gpreT_d = din("g_preT", [2, 128, 8])
    gpostT_d = din("g_postT", [2, 128, 8])
    win_d = din("w_in", [2, D, D_IN])
    rowp_d = din("rowp", [2, 128, NROW])
    convT_d = din("convT", [2, 128, 12, 3])
    sgwT_d = din("sg_wT", [2, 128, 4, 128])
    sgbT_d = din("sg_bT", [2, 128, 4])
    wbr_d = din("w_branch", [2, 4, 512, D])
    wout_d = din("w_out", [2, D, D])
    ident_d = din("ident", [128, 128])
    triF_d = din("triF", [128, 128])
    triB_d = din("triB", [128, 128])
    cos_d = din("ropec", [128, 16, 32])
    sin_d = din("ropes", [128, 16, 32])
    out_d = nc.dram_tensor("out", [2048, D], F32, kind="ExternalOutput").ap()
    dbg_kind = "ExternalOutput" if taps else "Internal"
    r1_d = nc.dram_tensor("r1", [NTOK, D], F32, kind=dbg_kind).ap()
    yg_d = nc.dram_tensor("ygs", [4, NTOK, 512], BF16, kind=dbg_kind).ap()
    tap_d = {}
    for nm, shp in taps:
        tap_d[nm] = nc.dram_tensor(nm, list(shp), F32, kind="ExternalOutput").ap()

    with contextlib.ExitStack() as gst:
        sems = {k: gst.enter_context(nc.semaphore("s_" + k)) for k in COMPUTE}
        sems["dma"] = {"sp": [gst.enter_context(nc.semaphore("d_sp%d" % i)) for i in range(8)],
                       "pool": [gst.enter_context(nc.semaphore("d_pl%d" % i)) for i in range(8)]}
        S = Sched(nc, sems)

        uniq = [0]

        def sbt(st, name, shape, dt=F32):
            uniq[0] += 1
            return st.enter_context(nc.sbuf_tensor("%s_%d" % (name, uniq[0]), list(shape), dt))

        def fsz(ap):
            n_ = 1
            for d_ in ap.shape[1:]:
                n_ *= d_
            return n_

        def MM(out, lhsT, rhs, start, stop, r, w):
            c_ = (64 + fsz(rhs)) / 2400.0 * (4.0 if lhsT.dtype == F32 else 1.0)
            S.pe(lambda e: e.matmul(out, lhsT=lhsT, rhs=rhs, start=start, stop=stop), r, w, cost=c_)

        def TR(out, in_, idn, r, w):
            S.pe(lambda e: e.transpose(out, in_, idn), r, w, cost=(0.3 if in_.dtype == F32 else 0.1))

        def ACT(out, in_, func, r, w, bias=None, scale=None, accum=None):
            kw = {}
            if bias is not None:
                kw["bias"] = bias
            if scale is not None:
                kw["scale"] = scale
            if accum is not None:
                kw["accum_out"] = accum
            S.act(lambda e: e.activation(out=out, in_=in_, func=func, **kw), r, w, cost=(224 + fsz(out)) / 1200.0)

        def TS(out, in0, s1, s2, op0, op1, r, w):
            c_ = (70 + fsz(out)) / 960.0
            if s2 is None:
                S.dve(lambda e: e.tensor_scalar(out=out, in0=in0, scalar1=s1, scalar2=None, op0=op0), r, w, cost=c_)
            else:
                S.dve(lambda e: e.tensor_scalar(out=out, in0=in0, scalar1=s1, scalar2=s2, op0=op0, op1=op1), r, w, cost=c_)

        def TT(out, in0, in1, op, r, w):
            S.dve(lambda e: e.tensor_tensor(out=out, in0=in0, in1=in1, op=op), r, w, cost=(70 + fsz(out)) / 960.0)

        def STT(out, in0, scalar, in1, op0, op1, r, w):
            S.dve(lambda e: e.scalar_tensor_tensor(out=out, in0=in0, scalar=scalar, in1=in1, op0=op0, op1=op1), r, w,
                  cost=(70 + fsz(out)) / 960.0)

        def CP(out, in_, r, w):
            S.dve(lambda e: e.tensor_copy(out=out, in_=in_), r, w, cost=(70 + fsz(out) * (0.5 if out.dtype == BF16 else 1.0)) / 960.0)

        def ACP(out, in_, r, w):
            S.act(lambda e: e.copy(out=out, in_=in_), r, w, cost=(224 + fsz(out)) / 1200.0)

        def RED(out, in_, op, r, w):
            S.dve(lambda e: e.tensor_reduce(out=out, in_=in_, axis=AX.X, op=op), r, w, cost=(70 + fsz(in_)) / 960.0)

        def RCP(out, in_, r, w):
            S.dve(lambda e: e.reciprocal(out=out, in_=in_), r, w, cost=(70 + fsz(out)) / 960.0)

        def PTS(out, in0, s1, r, w):
            S.add("pool", lambda e: e.tensor_scalar(out=out, in0=in0, scalar1=s1, scalar2=None, op0=ALU.mult), r, w, cost=(150 + fsz(out)) / 1000.0)

        def PTT(out, in0, in1, op, r, w):
            S.add("pool", lambda e: e.tensor_tensor(out=out, in0=in0, in1=in1, op=op), r, w, cost=(150 + fsz(out)) / 1000.0)

        def PSTT(out, in0, scalar, in1, op0, op1, r, w):
            S.add("pool", lambda e: e.scalar_tensor_tensor(out=out, in0=in0, scalar=scalar, in1=in1, op0=op0, op1=op1), r, w,
                  cost=(150 + fsz(out)) / 1000.0)

        def MSET(ap, v, w):
            S.dve(lambda e: e.memset(ap, v), (), w, cost=(70 + fsz(ap) * 0.5) / 960.0)

        def DMA(q, out, in_, r, w):
            nb = fsz(out) * 128 * (2 if out.dtype == BF16 else 4)
            S.dma(q, lambda e: e.dma_start(out=out, in_=in_), r, w, cost=2.0 + nb / 150e3)

        def rsqrt_to(out, in_, scale, r, w):
            ACT(out, in_, AF.Ln, r, w, bias=epsc[:, 0:1], scale=scale)
            ACT(out, out, AF.Exp, w, w, scale=-0.5)

        def sig_exp(out, psrc, pb, tmp, btmp, w, mul_x):
            ACT(tmp, psrc, AF.Exp, [pb], [btmp], scale=-1.0)
            ACT(tmp, tmp, AF.Ln, [btmp, bC], [btmp], bias=onec[:, 0:1])
            if mul_x:
                ACT(tmp, tmp, AF.Exp, [btmp], [btmp], scale=-1.0)
                TT(out, psrc, tmp, ALU.mult, [pb, btmp], w)
            else:
                ACT(out, tmp, AF.Exp, [btmp], w, scale=-1.0)

        psw = [gst.enter_context(nc.psum_tensor("psw%d" % i, [128, 1024], F32)) for i in range(4)]
        ps = [psw[i // 2][:, (i % 2) * 512:(i % 2 + 1) * 512] for i in range(8)]
        PB = [Buf("ps%d" % i, excl=True) for i in range(8)]
        ident = sbt(gst, "ident", [128, 128]); identb = sbt(gst, "identb", [128, 128], BF16)
        triF = sbt(gst, "triF", [128, 128]); triB = sbt(gst, "triB", [128, 128])
        triFb = sbt(gst, "triFb", [128, 128], BF16); triBb = sbt(gst, "triBb", [128, 128], BF16)
        ones = sbt(gst, "ones", [128, 128]); epsc = sbt(gst, "epsc", [128, 1]); onec = sbt(gst, "onec", [128, 1])
        inT = sbt(gst, "inT", [128, 8, TW], BF16)
        rowp = sbt(gst, "rowp", [128, NROW])
        modT = sbt(gst, "modT", [128, 24, 2]); aT = sbt(gst, "aT", [128, 8, 2]); gpT = sbt(gst, "gpT", [128, 8, 2])
        gprow = [sbt(gst, "gprow%d" % v, [128, D]) for v in range(2)]
        bC = Buf("consts"); bInT = [Buf("inT%d" % i) for i in range(NCH)]; bHalo = Buf("halo")
        bRow = Buf("rowp"); bMod = Buf("mod"); bGp = [Buf("gprow0"), Buf("gprow1")]
        bYG = [[Buf("yg%d_%d" % (k, ci)) for ci in range(NCH)] for k in range(4)]
        bRes = [[Buf("res%d_%d" % (l, ci)) for ci in range(NCH)] for l in range(3)]

        def ps_bf(i):
            return ps[i].bitcast(BF16)

        DMA("sp", ident[:], ident_d, (), [bC]); DMA("sp", triF[:], triF_d, (), [bC]); DMA("sp", triB[:], triB_d, (), [bC])
        CP(identb[:], ident[:], [bC], [bC]); CP(triFb[:], triF[:], [bC], [bC]); CP(triBb[:], triB[:], [bC], [bC])
        MSET(ones[:], 1.0, [bC]); MSET(epsc[:], EPS, [bC]); MSET(onec[:], 1.0, [bC])
        S.dve(lambda e: e.memset(inT[:], 0.0), (), [bHalo] + bInT)
        S.emit()

        def load_w(st, name, l, c0, n, q="pool", bounds=None, step=512):
            t = sbt(st, name, [128, 8, n], BF16)
            src = win_d[l].rearrange("(kc p) n -> p kc n", p=128)
            if bounds is None:
                bounds = [(a, min(a + step, n)) for a in range(0, n, step)]
            bufs = [Buf("%s_%d" % (name, i)) for i in range(len(bounds))]
            for i, (a, b) in enumerate(bounds):
                DMA(q, t[:, :, a:b], src[:, :, c0 + a:c0 + b], (), [bufs[i]])

            def wb(a, b):
                return [bufs[i] for i, (x, y) in enumerate(bounds) if x < b and y > a]
            return t, wb

        stopflag = [False]

        def chk(name):
            if stop == name:
                stopflag[0] = True
            return stopflag[0]

        def layer(l):
            last = (l == depth - 1)
            src_d = xin if l == 0 else r1_d
            chunks_out = list(range(2, NCH)) if last else list(range(NCH))
            stM = contextlib.ExitStack()
            if True:
                st = stM
                cc = sbt(st, "cc", [128, 8, 2]); sc = sbt(st, "sc", [128, 8, 2])
                bmT = sbt(st, "bmT", [128, 24]); gpre = sbt(st, "gpre", [128, 8]); gpost = sbt(st, "gpost", [128, 8])
                wm = [sbt(st, "wm%d" % i, [128, 8, 512], BF16) for i in range(3)]
                scb = sbt(st, "scb", [128, 8, 2], BF16)
                diag = [sbt(st, "diag%d" % i, [128, 128]) for i in range(2)]
                tmp82 = sbt(st, "tmp82", [128, 8, 2])
                bcc, bsc, bsm = Buf("cc"), Buf("sc"), Buf("small")
                bwm = [Buf("wm0"), Buf("wm1"), Buf("wm2")]; bdg = [Buf("dg0"), Buf("dg1")]
                DMA("sp", cc[:], cc_d, (), [bcc]); DMA("sp", bmT[:], bmodT_d[l], (), [bsm])
                DMA("sp", gpre[:], gpreT_d[l], (), [bsm]); DMA("sp", gpost[:], gpostT_d[l], (), [bsm])
                DMA("sp", rowp[:], rowp_d[l], (), [bRow])
                ACT(sc[:], cc[:], AF.Silu, [bcc], [bsc])
                CP(scb[:], sc[:], [bsc], [bsc])
                wsrc = wmod_d[l].rearrange("(kc p) n -> p kc n", p=128)
                for j in range(6):
                    DMA("pool", wm[j % 3][:], wsrc[:, :, j * 512:(j + 1) * 512], (), [bwm[j % 3]])
                    for sub in range(4):
                        cidx = j * 4 + sub
                        for kc in range(8):
                            MM(ps[0][:, cidx * 2:cidx * 2 + 2], wm[j % 3][:, kc, sub * 128:(sub + 1) * 128], scb[:, kc, :],
                               kc == 0, kc == 7, [bwm[j % 3], bsc], [PB[0]])
                TT(modT[:], ps[0][:, 0:48].rearrange("p (c v) -> p c v", v=2), bmT[:].unsqueeze(2).broadcast_to([128, 24, 2]),
                   ALU.add, [PB[0], bsm], [bMod])
                TS(tmp82[:], modT[:, 8:16, :], 1.0, None, ALU.add, None, [bMod], [bsm])
                TT(aT[:], tmp82[:], gpre[:].unsqueeze(2).broadcast_to([128, 8, 2]), ALU.mult, [bsm], [bMod])
                TT(gpT[:], modT[:, 16:24, :], gpost[:].unsqueeze(2).broadcast_to([128, 8, 2]), ALU.mult, [bMod, bsm], [bMod])
                for v in range(2):
                    for fc in range(8):
                        dg = diag[fc % 2]
                        TS(dg[:], ident[:], gpT[:, fc, v:v + 1], None, ALU.mult, None, [bMod, bC], [bdg[fc % 2]])
                        bank = 1 + fc // 4
                        MM(ps[bank][:, (fc % 4) * 128:(fc % 4 + 1) * 128], ones[:], dg[:], True, True, [bC, bdg[fc % 2]], [PB[bank]])
                    ACP(gprow[v][:, 0:512], ps[1][:], [PB[1]], [bGp[v]])
                    ACP(gprow[v][:, 512:1024], ps[2][:], [PB[2]], [bGp[v]])
            with contextlib.ExitStack() as st:
                xc = [sbt(st, "xc%d" % i, [128, D]) for i in range(2)]
                xs = [sbt(st, "xs%d" % i, [128, D]) for i in range(2)]
                junk = sbt(st, "junk", [128, D], BF16)
                ss = [sbt(st, "ss%d" % i, [128, 1]) for i in range(2)]
                bxc = [Buf("xc0"), Buf("xc1")]; bxs = [Buf("xs0"), Buf("xs1")]; bj = Buf("junk"); bss = [Buf("ss0"), Buf("ss1")]
                for ci in range(NCH):
                    i2 = ci % 2
                    v = 1 if ci < 2 else 0
                    DMA("sp", xc[i2][:], src_d[ci * 128:(ci + 1) * 128, :], [bRes[l][ci]], [bxc[i2]])
                    MSET(ss[i2][:], 0.0, [bss[i2]])
                    ACT(junk[:], xc[i2][:], AF.Square, [bxc[i2], bss[i2]], [bj, bss[i2]], accum=ss[i2][:])
                    rsqrt_to(ss[i2][:], ss[i2][:], 1.0 / D, [bss[i2], bC], [bss[i2]])
                    TS(xs[i2][:], xc[i2][:], ss[i2][:, 0:1], None, ALU.mult, None, [bxc[i2], bss[i2]], [bxs[i2]])
                    for fc in range(8):
                        bank = 3 + fc // 4 + 2 * i2
                        TR(ps[bank][:, (fc % 4) * 128:(fc % 4 + 1) * 128], xs[i2][:, fc * 128:(fc + 1) * 128], ident[:],
                           [bxs[i2], bC], [PB[bank]])
                    for fc in range(8):
                        bank = 3 + fc // 4 + 2 * i2
                        if fc % 2 == 0:
                            TS(inT[:, fc, col(ci):col(ci) + 128], ps[bank][:, (fc % 4) * 128:(fc % 4 + 1) * 128],
                               aT[:, fc, v:v + 1], modT[:, fc, v:v + 1], ALU.mult, ALU.add, [PB[bank], bMod], [bInT[ci]])
                        else:
                            ACT(inT[:, fc, col(ci):col(ci) + 128], ps[bank][:, (fc % 4) * 128:(fc % 4 + 1) * 128], AF.Identity,
                                [PB[bank], bMod], [bInT[ci]], bias=modT[:, fc, v:v + 1], scale=aT[:, fc, v:v + 1])
                S.emit()
            stM.close()
            if "inT" in tap_d and l == 0:
                with contextlib.ExitStack() as st:
                    tt = sbt(st, "tapt", [128, 8, TW]); bt = Buf("tapt")
                    CP(tt[:], inT[:], bInT + [bHalo], [bt])
                    DMA("sp", tap_d["inT"], tt[:], [bt], ())
                    S.emit()
            if chk("N"):
                return

            def phase_B(st, BB):
                W, bW = load_w(st, "WB", l, 2560, 1536)
                swf = sbt(st, "swf", [128, 4, 128]); swb = sbt(st, "swb", [128, 4, 128], BF16); sbT = sbt(st, "sbT", [128, 4])
                bsw = Buf("sgw")
                NB_ = 3
                ug = [sbt(st, "ug%d" % i, [128, 512]) for i in range(NB_)]; bug = [Buf("ug%d" % i) for i in range(NB_)]
                vg_ = [sbt(st, "vg%d" % i, [128, 512]) for i in range(NB_)]; xcn_ = [sbt(st, "xcn%d" % i, [128, 512]) for i in range(NB_)]
                vn_ = [sbt(st, "vn%d" % i, [128, 512], BF16) for i in range(NB_)]
                junk = sbt(st, "junkb", [128, 512], BF16)
                sv_ = [sbt(st, "sv%d" % i, [128, 2]) for i in range(NB_)]; bv_ = [Buf("vwork%d" % i) for i in range(NB_)]
                sgt = [sbt(st, "sgt%d" % i, [128, 512]) for i in range(NB_)]; bsgt = [Buf("sgt%d" % i) for i in range(NB_)]
                mx_ = [sbt(st, "mx%d" % i, [128, 512]) for i in range(NB_)]; ygb = [sbt(st, "ygb%d" % i, [128, 512], BF16) for i in range(NB_)]
                bmx_ = [Buf("mx%d" % i) for i in range(NB_)]; bygb = [Buf("ygb%d" % i) for i in range(NB_)]
                bjb = Buf("junkb")
                DMA("sp", swf[:], sgwT_d[l], (), [bsw]); DMA("sp", sbT[:], sgbT_d[l], (), [bsw])
                CP(swb[:], swf[:], [bsw], [bsw])
                for n_i, ci in enumerate(chunks_out):
                    i2 = n_i % NB_
                    pb0 = 0
                    vg, xcn, vn, sv, bv, mx, bmx = vg_[i2], xcn_[i2], vn_[i2], sv_[i2], bv_[i2], mx_[i2], bmx_[i2]
                    cs = slice(col(ci), col(ci) + 128)
                    for kc in range(8):
                        MM(ps[BB[0]][:], inT[:, kc, cs], W[:, kc, 0:512], kc == 0, kc == 7, [bInT[ci]] + bW(0, 512), [PB[BB[0]]])
                    ACT(ug[i2][:], ps[BB[0]][:], AF.Gelu_apprx_tanh, [PB[BB[0]]], [bug[i2]])
                    for kc in range(8):
                        MM(ps[BB[1]][:], inT[:, kc, cs], W[:, kc, 512:1024], kc == 0, kc == 7, [bInT[ci]] + bW(512, 1024), [PB[BB[1]]])
                    MSET(sv[:], 0.0, [bv])
                    ACT(vg[:], ps[BB[1]][:], AF.Gelu_apprx_tanh, [PB[BB[1]], bv], [bv], accum=sv[:, 0:1])
                    TS(sv[:, 0:1], sv[:, 0:1], -1.0 / 512, None, ALU.mult, None, [bv], [bv])
                    TS(xcn[:], vg[:], sv[:, 0:1], None, ALU.add, None, [bv], [bv])
                    ACT(junk[:], xcn[:], AF.Square, [bv], [bv, bjb], accum=sv[:, 1:2])
                    rsqrt_to(sv[:, 1:2], sv[:, 1:2], 1.0 / 512, [bv, bC], [bv])
                    STT(xcn[:], xcn[:], sv[:, 1:2], rowp[:, R_LNG:R_LNG + 512], ALU.mult, ALU.mult, [bv, bRow], [bv])
                    TT(vn[:], xcn[:], rowp[:, R_LNB:R_LNB + 512], ALU.add, [bv, bRow], [bv])
                    for g in range(4):
                        MM(ps[BB[2]][:, g * 128:(g + 1) * 128], swb[:, g, :], vn[:, g * 128:(g + 1) * 128], True, True, [bsw, bv], [PB[BB[2]]])
                    for kc in range(8):
                        MM(ps[BB[3]][:], inT[:, kc, cs], W[:, kc, 1024:1536], kc == 0, kc == 7, [bInT[ci]] + bW(1024, 1536), [PB[BB[3]]])
                    sig_exp(sgt[i2][:], ps[BB[3]][:], PB[BB[3]], sgt[i2][:], bsgt[i2], [bsgt[i2]], True)
                    TT(mx[:].rearrange("p (g c) -> p g c", g=4), ps[BB[2]][:].rearrange("p (g c) -> p g c", g=4),
                       sbT[:].unsqueeze(2).broadcast_to([128, 4, 128]), ALU.add, [PB[BB[2]], bsw], [bmx])
                    TT(mx[:], mx[:], ug[i2][:], ALU.mult, [bmx, bug[i2]], [bmx])
                    TT(ygb[i2][:], mx[:], sgt[i2][:], ALU.mult, [bmx, bsgt[i2]], [bygb[i2]])
                    DMA("sp", yg_d[2, ci * 128:(ci + 1) * 128, :], ygb[i2][:], [bygb[i2]], [bYG[2][ci]])

            for br, c0 in ((0, 0), (1, 1280)):
                is_win = (br == 1)
                with contextlib.ExitStack() as st:
                    W, bW = load_w(st, "WA", l, c0, 1280, bounds=[(512, 768), (0, 512), (768, 1280)])
                    ropec = sbt(st, "ropec", [128, 16, 32]); ropes = sbt(st, "ropes", [128, 16, 32]); bRp = Buf("rope")
                    DMA("sp", ropec[:], cos_d, (), [bRp]); DMA("sp", ropes[:], sin_d, (), [bRp])
                    kT = sbt(st, "kT", [128, 2, NTOK], BF16); bkT = Buf("kT")
                    vaug = sbt(st, "vaug", [128, NCH, 2, 68], BF16); bV = Buf("vaug")
                    PT = [sbt(st, "PT%d" % i, [128, 6 if is_win else 18, 512], BF16) for i in range(2)]; bPT = [Buf("PT0"), Buf("PT1")]
                    NQB = 2 if is_win else 3
                    STW = [1] if is_win else [1, 2]
                    BK_PVT, BK_RES = (4, 5) if is_win else (6, 7)
                    qf_ = [sbt(st, "qf%d" % i, [128, 512]) for i in range(NQB)]; qsq_ = [sbt(st, "qsq%d" % i, [128, 512]) for i in range(NQB)]
                    qn_ = [sbt(st, "qn%d" % i, [128, 512]) for i in range(NQB)]; qr_ = [sbt(st, "qr%d" % i, [128, 512], BF16) for i in range(NQB)]
                    rt_ = [[sbt(st, "rt%d_%d" % (b_, i), [128, 8, 32]) for i in range(4)] for b_ in range(NQB)]
                    ssq_ = [sbt(st, "ssq%d" % i, [128, 8]) for i in range(NQB)]
                    bq_ = [Buf("qwork%d" % i) for i in range(NQB)]
                    qcnt = [0]
                    qT = [sbt(st, "qT%d" % i, [128, 8, 128], BF16) for i in range(2)]; bqT = [Buf("qT0"), Buf("qT1")]
                    S.dve(lambda e: e.memset(kT[:], 0.0), (), [bkT], cost=5.0)
                    MSET(qT[0][:], 0.0, [bqT[0]]); MSET(qT[1][:], 0.0, [bqT[1]])
                    sg = [sbt(st, "sg%d" % i, [128, 512]) for i in range(2)]; bsg = [Buf("sg0"), Buf("sg1")]
                    es = sbt(st, "es", [128, 8]); den = sbt(st, "den", [128, 4]); ya = sbt(st, "ya", [128, 256])
                    sgtmp = sbt(st, "sgtmp", [128, 512]); bsgtmp = Buf("sgtmp")
                    oT = [sbt(st, "oT%d" % i, [68, 512]) for i in range(2)]; boT = [Buf("oT0"), Buf("oT1")]
                    yga = [sbt(st, "yga%d" % i, [128, 512], BF16) for i in range(2)]; byga = [Buf("yga0"), Buf("yga1")]
                    bes = Buf("es"); bden = Buf("den")
                    MSET(vaug[:, :, :, 64:68], 1.0, [bV])
                    if is_win:
                        ACT(es[:], rowp[:, R_SINK:R_SINK + 8], AF.Exp, [bRow], [bes])

                    def qk_prep(psrc, nh, ci, is_q, pb):
                        w = nh * 64
                        bi = qcnt[0] % NQB
                        qcnt[0] += 1
                        qf, qsq, qn, qr, rt, ssq, bq = qf_[bi], qsq_[bi], qn_[bi], qr_[bi], rt_[bi], ssq_[bi], bq_[bi]
                        v3 = lambda t: t[:, 0:w].rearrange("p (h d) -> p h d", d=64)
                        ACP(qf[:, 0:w], psrc, [pb], [bq])
                        cur = qf
                        if not is_win:
                            TT(qsq[:, 0:w], qf[:, 0:w], qf[:, 0:w], ALU.mult, [bq], [bq])
                            RED(ssq[:, 0:nh], v3(qsq), ALU.add, [bq], [bq])
                            rsqrt_to(ssq[:, 0:nh], ssq[:, 0:nh], 1.0 / 64, [bq, bC], [bq])
                            if is_q:
                                TS(ssq[:, 0:nh], ssq[:, 0:nh], 0.125, None, ALU.mult, None, [bq], [bq])
                            TT(v3(qn), v3(qf), ssq[:, 0:nh].unsqueeze(2).broadcast_to([128, nh, 64]), ALU.mult, [bq], [bq])
                            r0 = R_QN if is_q else R_KN
                            TT(v3(qn), v3(qn), rowp[:, r0:r0 + 64].unsqueeze(1).broadcast_to([128, nh, 64]), ALU.mult, [bq, bRow], [bq])
                            cur = qn
                        elif is_q:
                            TS(qn[:, 0:w], qf[:, 0:w], 0.125, None, ALU.mult, None, [bq], [bq])
                            cur = qn
                        if ci >= 2:
                            j = ci - 2
                            cb = ropec[:, j, :].unsqueeze(1).broadcast_to([128, nh, 32])
                            sb_ = ropes[:, j, :].unsqueeze(1).broadcast_to([128, nh, 32])
                            x1 = v3(cur)[:, :, 0:32]; x2 = v3(cur)[:, :, 32:64]
                            t = [r[:, 0:nh, :] for r in rt]
                            TT(t[0], x1, cb, ALU.mult, [bq, bRp], [bq]); TT(t[1], x2, sb_, ALU.mult, [bq, bRp], [bq])
                            TT(v3(qr)[:, :, 0:32], t[0], t[1], ALU.subtract, [bq], [bq])
                            TT(t[2], x1, sb_, ALU.mult, [bq, bRp], [bq]); TT(t[3], x2, cb, ALU.mult, [bq, bRp], [bq])
                            TT(v3(qr)[:, :, 32:64], t[2], t[3], ALU.add, [bq], [bq])
                        else:
                            CP(qr[:, 0:w], cur[:, 0:w], [bq], [bq])
                        return qr, bq

                    for ci in range(NCH):
                        bank = ci % 2
                        for kc in range(8):
                            MM(ps[bank][:, 0:256], inT[:, kc, col(ci):col(ci) + 128], W[:, kc, 512:768], kc == 0, kc == 7,
                               [bInT[ci]] + bW(512, 768), [PB[bank]])
                        CP(vaug[:, ci, :, 0:64], ps[bank][:, 128:256].rearrange("p (g d) -> p g d", d=64), [PB[bank]], [bV])
                        qr, bq = qk_prep(ps[bank][:, 0:128], 2, ci, False, PB[bank])
                        tb_ = 2 + bank
                        for g in range(2):
                            TR(ps_bf(tb_)[0:64, g * 128:(g + 1) * 128], qr[:, g * 64:(g + 1) * 64], identb[:], [bq, bC], [PB[tb_]])
                        CP(kT[0:64, :, ci * 128:(ci + 1) * 128], ps_bf(tb_)[0:64, 0:256].rearrange("p (g t) -> p g t", g=2), [PB[tb_]], [bkT])
                    if stop == "A1":
                        S.emit()
                        if chk("A1"):
                            return
                    for n_i, ci in enumerate(chunks_out):
                        i2 = n_i % 2
                        for kc in range(8):
                            MM(ps[0][:], inT[:, kc, col(ci):col(ci) + 128], W[:, kc, 0:512], kc == 0, kc == 7, [bInT[ci]] + bW(0, 512), [PB[0]])
                        qr, bq = qk_prep(ps[0][:], 8, ci, True, PB[0])
                        for h in range(8):
                            TR(ps_bf(1)[0:64, h * 128:(h + 1) * 128], qr[:, h * 64:(h + 1) * 64], identb[:], [bq, bC], [PB[1]])
                        CP(qT[i2][0:64].rearrange("p h t -> p (h t)"), ps_bf(1)[0:64, :], [PB[1]], [bqT[i2]])
                        for kc in range(8):
                            MM(ps[0][:], inT[:, kc, col(ci):col(ci) + 128], W[:, kc, 768:1280], kc == 0, kc == 7, [bInT[ci]] + bW(768, 1280), [PB[0]])
                        sig_exp(sg[i2][:], ps[0][:], PB[0], sgtmp[:], bsgtmp, [bsg[i2]], True)
                        if ci < 2:
                            keys = [(0, None), (1, None)]
                        elif not is_win:
                            keys = [(k, None) for k in range(NCH)]
                        else:
                            keys = [(0, None), (1, None)]
                            if ci - 1 >= 2:
                                keys.append((ci - 1, triBb))
                            keys.append((ci, None))
                            if ci + 1 < NCH:
                                keys.append((ci + 1, triFb))
                        for g in range(2):
                            pt = PT[g]; bpt = bPT[g]
                            for p0_ in range(0, len(keys), 2):
                                pair = keys[p0_:p0_ + 2]
                                wsel = STW[(p0_ // 2) % len(STW)]
                                bks = [2 * wsel, 2 * wsel + 1]
                                for j_, (kc_i, msk) in enumerate(pair):
                                    MM(ps[bks[j_]][:], kT[:, g, kc_i * 128:(kc_i + 1) * 128], qT[i2][:, 4 * g:4 * g + 4, :].rearrange("p h t -> p (h t)"),
                                       True, True, [bkT, bqT[i2]], [PB[bks[j_]]])
                                n_ = len(pair)
                                ACT(pt[:, p0_:p0_ + n_, :].rearrange("p k t -> p (k t)"), psw[wsel][:, 0:512 * n_], AF.Exp,
                                    [PB[b_] for b_ in bks[:n_]], [bpt])
                                for j_, (kc_i, msk) in enumerate(pair):
                                    ki = p0_ + j_
                                    if msk is not None:
                                        TT(pt[:, ki, :].rearrange("p (h t) -> p h t", h=4), pt[:, ki, :].rearrange("p (h t) -> p h t", h=4),
                                           msk[:].unsqueeze(1).broadcast_to([128, 4, 128]), ALU.mult, [bpt, bC], [bpt])
                            ob = BK_PVT
                            for ki, (kc_i, msk) in enumerate(keys):
                                MM(ps[ob][0:68, :], vaug[:, kc_i, g, :], pt[:, ki, :], ki == 0, ki == len(keys) - 1, [bpt, bV], [PB[ob]])
                            ACP(oT[g][:], ps[ob][0:68, :], [PB[ob]], [boT[g]])
                            for h in range(4):
                                TR(ps[BK_RES][:, h * 80:h * 80 + 68], oT[g][:, h * 128:(h + 1) * 128], ident[0:68, 0:68], [boT[g], bC], [PB[BK_RES]])
                            ob = BK_RES
                            o3 = ps[ob][:, 0:320].rearrange("p (h d) -> p h d", d=80)
                            if is_win:
                                TT(den[:], o3[:, :, 64], es[:, 4 * g:4 * g + 4], ALU.add, [PB[ob], bes], [bden])
                            else:
                                CP(den[:], o3[:, :, 64], [PB[ob]], [bden])
                            RCP(den[:], den[:], [bden], [bden])
                            TT(ya[:].rearrange("p (h d) -> p h d", d=64), o3[:, :, 0:64], den[:].unsqueeze(2).broadcast_to([128, 4, 64]),
                               ALU.mult, [PB[ob], bden], [bden])
                            TT(yga[i2][:, g * 256:(g + 1) * 256], ya[:], sg[i2][:, g * 256:(g + 1) * 256], ALU.mult,
                               [bden, bsg[i2]], [byga[i2]])
                        DMA("sp", yg_d[br, ci * 128:(ci + 1) * 128, :], yga[i2][:], [byga[i2]], [bYG[br][ci]])
                    if is_win:
                        phase_B(st, [6, 7, 6, 7])
                    S.emit()
                    if chk("AC"[br]):
                        return


            with contextlib.ExitStack() as stD:
              with contextlib.ExitStack() as st:
                qTm = sbt(stD, "qTm", [128, 4, NTOK], BF16); kTm = sbt(stD, "kTm", [128, 4, NTOK], BF16)
                ktm = sbt(stD, "ktm", [128, NCH, 512], BF16); vau = sbt(stD, "vau", [128, NCH, 4, 132], BF16)
                lf = sbt(stD, "lf", [128, NCH, 8]); li = sbt(stD, "li", [128, NCH, 8]); gif = sbt(stD, "gif", [128, NCH, 16])
                bQ, bK, bKt, bVa, bG = Buf("qTm"), Buf("kTm"), Buf("ktm"), Buf("vau"), Buf("gates")
                MSET(vau[:, :, :, 128:132], 1.0, [bVa])
                with contextlib.ExitStack() as st1:
                    groups = [(0, 1, 256)] + [(2 + 4 * i, 259 + 512 * i, 512) for i in range(4)]
                    cwT = sbt(st1, "cwT", [128, 12, 3]); bcw = Buf("cwT")
                    DMA("sp", cwT[:], convT_d[l], (), [bcw])
                    NPF = 3
                    Pf = [sbt(st1, "Pf%d" % i, [128, TW]) for i in range(NPF)]; bPf = [Buf("Pf%d" % i) for i in range(NPF)]
                    zcs = [sbt(st1, "zc%d" % i, [128, TW]) for i in range(2)]; bzs = [Buf("zc0"), Buf("zc1")]
                    vT = sbt(st1, "vT", [128, NTOK], BF16); bvT = Buf("vT")
                    for i in range(NPF):
                        S.dve(lambda e, t=Pf[i]: e.memset(t[:], 0.0), (), [bPf[i]], cost=2.5)
                    cnt = 0
                    Wg, bWg = load_w(st1, "Wif", l, 5632, 16)
                    Wrs = [load_w(st1, "Wr%d" % si_, l, 4096 + 512 * si_, 512, step=128) for si_ in range(3)]
                    for si, sname in enumerate(("q", "k", "v")):
                        if True:
                            Wr, bWr = Wrs[si]
                            for h in range(4):
                                pf = Pf[(si * 4 + h) % NPF]; bpf = bPf[(si * 4 + h) % NPF]
                                zc = zcs[(si * 4 + h) % 2]; bz = bzs[(si * 4 + h) % 2]
                                for (c_first, c_col, nt) in groups:
                                    bank = cnt % 2; cnt += 1
                                    rb = [bInT[c] for c in range(c_first, c_first + nt // 128)]
                                    for kc in range(8):
                                        MM(ps[bank][:, 0:nt], Wr[:, kc, h * 128:(h + 1) * 128], inT[:, kc, c_col:c_col + nt],
                                           kc == 0, kc == 7, rb + bWr(h * 128, (h + 1) * 128), [PB[bank]])
                                    ACP(pf[:, c_col:c_col + nt], ps[bank][:, 0:nt], [PB[bank]], [bpf])
                                cidx = si * 4 + h
                                zz = zc[:, 1:2307]
                                TS(zz, pf[:, 0:2306], cwT[:, cidx, 0:1], None, ALU.mult, None, [bpf, bcw], [bz])
                                STT(zz, pf[:, 1:2307], cwT[:, cidx, 1:2], zz, ALU.mult, ALU.add, [bpf, bcw, bz], [bz])
                                STT(zz, pf[:, 2:2308], cwT[:, cidx, 2:3], zz, ALU.mult, ALU.add, [bpf, bcw, bz], [bz])
                                if sname == "q":
                                    dst, bd = qTm[:, h, :], bQ
                                elif sname == "k":
                                    dst, bd = kTm[:, h, :], bK
                                else:
                                    dst, bd = vT[:, :], bvT
                                ACT(dst[:, 0:256], zc[:, 1:257], AF.Silu, [bz], [bd])
                                ACT(dst[:, 256:NTOK], zc[:, 259:2307], AF.Silu, [bz], [bd])
                                if sname in ("k", "v"):
                                    for c0_, nb in ((0, 8), (8, 8), (16, 2)):
                                        bank = 2 + cnt % 2; cnt += 1
                                        for j in range(nb):
                                            TR(ps_bf(bank)[:, j * 128:(j + 1) * 128], dst[:, (c0_ + j) * 128:(c0_ + j + 1) * 128], identb[:],
                                               [bd, bC], [PB[bank]])
                                        srcp = ps_bf(bank)[:, 0:nb * 128].rearrange("p (c t) -> p c t", c=nb)
                                        if sname == "k":
                                            CP(ktm[:, c0_:c0_ + nb, h * 128:(h + 1) * 128], srcp, [PB[bank]], [bKt])
                                        else:
                                            CP(vau[:, c0_:c0_ + nb, h, 0:128], srcp, [PB[bank]], [bVa])
                    with contextlib.ExitStack() as st2:
                        for ci in range(NCH):
                            bank = 4 + ci % 2
                            for kc in range(8):
                                MM(ps[bank][:, 0:16], inT[:, kc, col(ci):col(ci) + 128], Wg[:, kc, :], kc == 0, kc == 7, [bInT[ci]] + bWg(0, 16), [PB[bank]])
                            CP(gif[:, ci, :], ps[bank][:, 0:16], [PB[bank]], [bG])
                        TT(li[:], gif[:, :, 0:8], rowp[:, R_BI:R_BI + 8].unsqueeze(1).broadcast_to([128, NCH, 8]), ALU.add, [bG, bRow], [bG])
                        TT(lf[:], gif[:, :, 8:16], rowp[:, R_BF:R_BF + 8].unsqueeze(1).broadcast_to([128, NCH, 8]), ALU.add, [bG, bRow], [bG])
                        ACT(lf[:], lf[:], AF.Exp, [bG], [bG], scale=-1.0)
                        ACT(lf[:], lf[:], AF.Ln, [bG, bC], [bG], bias=onec[:, 0:1])
                        TS(lf[:], lf[:], -1.0, None, ALU.mult, None, [bG], [bG])
                        S.emit()
                        if chk("D1"):
                            return
                hsum = sbt(stD, "hsum", [128, NCH, 512]); bH = [Buf("hsum%d" % c) for c in range(NCH)]
                bball = sbt(st, "bball", [128, NCH, 16]); u_all = sbt(st, "u_all", [128, NCH, 8]); U_all = sbt(st, "U_all", [128, NCH, 8])
                bG2 = Buf("bball"); bU = Buf("U_all")
                dgu = [sbt(st, "dgu%d" % i, [128, 128]) for i in range(4)]; bdgu = [Buf("dgu%d" % i) for i in range(4)]
                for ci in range(NCH):
                    bk = ci % 2
                    MM(ps[bk][:, 0:4], triF[:], lf[:, ci, 0:4], True, True, [bC, bG], [PB[bk]])
                    MM(ps[bk][:, 4:8], triB[:], lf[:, ci, 4:8], True, True, [bC, bG], [PB[bk]])
                    MM(ps[bk][:, 8:16], ones[:], lf[:, ci, :], True, True, [bC, bG], [PB[bk]])
                    CP(bball[:, ci, :], ps[bk][:, 0:16], [PB[bk]], [bG2])
                TT(u_all[:], li[:], bball[:, :, 0:8], ALU.subtract, [bG, bG2], [bG2])
                u2 = u_all[:].rearrange("p c e -> p (c e)")
                U2 = U_all[:].rearrange("p c e -> p (c e)")
                umT = sbt(st, "umT", [72, 2]); bUm = Buf("umT")
                for hf in range(2):
                    TR(ps[2 + hf][0:72, 0:128], u2[:, hf * 72:(hf + 1) * 72], ident[:], [bG2, bC], [PB[2 + hf]])
                    RED(umT[:, hf:hf + 1], ps[2 + hf][0:72, 0:128], ALU.max, [PB[2 + hf]], [bUm])
                    TS(dgu[hf][0:72, 0:72], ident[0:72, 0:72], umT[:, hf:hf + 1], None, ALU.mult, None, [bUm, bC], [bdgu[hf]])
                    MM(ps[4 + hf][:, 0:72], ones[0:72, :], dgu[hf][0:72, 0:72], True, True, [bC, bdgu[hf]], [PB[4 + hf]])
                    CP(U2[:, hf * 72:(hf + 1) * 72], ps[4 + hf][:, 0:72], [PB[4 + hf]], [bU])
                CT = [sbt(st, "CT%d" % d, [128, 4, 132]) for d in range(2)]
                bCT = [[Buf("CT%d_%d" % (d, h)) for h in range(4)] for d in range(2)]
                mst = [sbt(st, "mst%d" % d, [128, 4]) for d in range(2)]; bm = [Buf("mst0"), Buf("mst1")]
                CTs = [[sbt(st, "CTs%d_%d" % (d, i), [128, 132], BF16) for i in range(2)] for d in range(2)]
                bCTs = [[Buf("CTs%d_%d" % (d, i)) for i in range(2)] for d in range(2)]
                kw = [[sbt(st, "kw%d_%d" % (d, i), [128, 128], BF16) for i in range(2)] for d in range(2)]
                bkw = [[Buf("kw%d_%d" % (d, i)) for i in range(2)] for d in range(2)]
                sT = [[sbt(st, "sT%d_%d" % (d, i), [128, 128], BF16) for i in range(2)] for d in range(2)]
                bsT = [[Buf("sT%d_%d" % (d, i)) for i in range(2)] for d in range(2)]
                dn = [[sbt(st, "dn%d_%d" % (d, i), [128, 1]) for i in range(2)] for d in range(2)]
                bdn = [[Buf("dn%d_%d" % (d, i)) for i in range(2)] for d in range(2)]
                Mxs = [[sbt(st, "Mx%d_%d" % (d, i), [128, 4]) for i in range(2)] for d in range(2)]
                e12s = [[sbt(st, "e12%d_%d" % (d, i), [128, 12]) for i in range(2)] for d in range(2)]
                ecs = [[sbt(st, "ec%d_%d" % (d, i), [128, 8]) for i in range(2)] for d in range(2)]
                bscs = [[Buf("scal%d_%d" % (d, i)) for i in range(2)] for d in range(2)]
                hb = [[sbt(st, "hb%d_%d" % (d, i), [128, 128]) for i in range(2)] for d in range(2)]
                bhb = [[Buf("hb%d_%d" % (d, i)) for i in range(2)] for d in range(2)]
                tmpC = [[sbt(st, "tmpC%d_%d" % (d, i), [128, 132]) for i in range(2)] for d in range(2)]
                btmpC = [[Buf("tmpC%d_%d" % (d, i)) for i in range(2)] for d in range(2)]
                CSC = 128.0 ** -0.5
                orders = [list(range(NCH)), [1, 0] + list(range(NCH - 1, 1, -1))]
                for d in range(2):
                    MSET(CT[d][:], 0.0, bCT[d]); MSET(mst[d][:], 0.0, [bm[d]])
                seen = set()
                for step in range(NCH):
                    for dr in range(2):
                        ci = orders[dr][step]
                        first = ci not in seen
                        seen.add(ci)
                        trib = triFb if dr == 0 else triBb
                        cs8 = slice(dr * 4, dr * 4 + 4)
                        tk = slice(ci * 128, (ci + 1) * 128)
                        Mx = Mxs[dr][step % 2]; e12 = e12s[dr][step % 2]; ec = ecs[dr][step % 2]; bsc_ = bscs[dr][step % 2]
                        TT(Mx[:], mst[dr][:], U_all[:, ci, cs8], ALU.max, [bm[dr], bU], [bsc_])
                        TT(e12[:, 0:4], mst[dr][:], Mx[:], ALU.subtract, [bm[dr], bsc_], [bsc_])
                        TT(e12[:, 4:8], u_all[:, ci, cs8], Mx[:], ALU.subtract, [bG2, bsc_], [bsc_])
                        STT(e12[:, 8:12], bball[:, ci, cs8], -1.0, Mx[:], ALU.mult, ALU.subtract, [bG2, bsc_], [bsc_])
                        ACT(e12[:], e12[:], AF.Exp, [bsc_], [bsc_])
                        TS(ec[:], e12[:, 0:8], CSC, None, ALU.mult, None, [bsc_], [bsc_])
                        TT(mst[dr][:], bball[:, ci, 8 + dr * 4:12 + dr * 4], Mx[:], ALU.add, [bG2, bsc_], [bm[dr]])
                        for h in range(4):
                            i2 = h % 2
                            ACT(CTs[dr][i2][:], CT[dr][:, h, :], AF.Identity, [bCT[dr][h], bsc_], [bCTs[dr][i2]], scale=ec[:, h:h + 1])
                            ACT(kw[dr][i2][:], ktm[:, ci, h * 128:(h + 1) * 128], AF.Identity, [bKt, bsc_], [bkw[dr][i2]], scale=e12[:, 4 + h:5 + h])
                            b1 = 4 * dr
                            qk = ps[b1][:, i2 * 128:(i2 + 1) * 128]
                            MM(qk, kTm[:, h, tk], qTm[:, h, tk], True, True, [bK, bQ], [PB[b1]])
                            STT(sT[dr][i2][:], qk, ec[:, 4 + h:5 + h], trib[:], ALU.mult, ALU.mult, [PB[b1], bsc_, bC], [bsT[dr][i2]])
                            b2 = 4 * dr + 1 + i2
                            MM(ps[b2][:, 0:132], sT[dr][i2][:], vau[:, ci, h, :], True, False, [bsT[dr][i2], bVa], [PB[b2]])
                            MM(ps[b2][:, 0:132], qTm[:, h, tk], CTs[dr][i2][:], False, True, [bQ, bCTs[dr][i2]], [PB[b2]])
                            dnn = dn[dr][i2]; bdnn = bdn[dr][i2]
                            ACT(dnn[:], ps[b2][:, 128:129], AF.Abs, [PB[b2]], [bdnn])
                            TT(dnn[:], dnn[:], e12[:, 8 + h:9 + h], ALU.max, [bdnn, bsc_], [bdnn])
                            RCP(dnn[:], dnn[:], [bdnn], [bdnn])
                            hs = hsum[:, ci, h * 128:(h + 1) * 128]
                            if first:
                                ACT(hs, ps[b2][:, 0:128], AF.Identity, [PB[b2], bdnn], [bH[ci]], scale=dnn[:, 0:1])
                            else:
                                STT(hs, ps[b2][:, 0:128], dnn[:, 0:1], hs, ALU.mult, ALU.add, [PB[b2], bdnn, bH[ci]], [bH[ci]])
                            b3 = 4 * dr + 3
                            cp_ = ps[b3][:, i2 * 256:i2 * 256 + 132]
                            MM(cp_, kw[dr][i2][:], vau[:, ci, h, :], True, True, [bkw[dr][i2], bVa], [PB[b3]])
                            STT(CT[dr][:, h, :], CT[dr][:, h, :], e12[:, h:h + 1], cp_, ALU.mult, ALU.add, [bCT[dr][h], bsc_, PB[b3]], [bCT[dr][h]])
                S.emit()
                if chk("D2"):
                    return
              if True:
                with contextlib.ExitStack() as st2:
                    W, bW = load_w(st2, "Wog", l, 5648, 1024)
                    so = [sbt(st2, "so%d" % i, [128, 512]) for i in range(2)]; bso = [Buf("so0"), Buf("so1")]
                    sgm = [sbt(st2, "sgm%d" % i, [128, 512]) for i in range(2)]; bsgm = [Buf("sgm0"), Buf("sgm1")]
                    hsq_ = [sbt(st2, "hsq%d" % i, [128, 512]) for i in range(2)]; hn_ = [sbt(st2, "hn%d" % i, [128, 512]) for i in range(2)]
                    s4_ = [sbt(st2, "s4%d" % i, [128, 4]) for i in range(2)]; bh_ = [Buf("hwork0"), Buf("hwork1")]
                    stmp = [sbt(st2, "stmp%d" % i, [128, 512]) for i in range(2)]; bstmp = [Buf("stmp0"), Buf("stmp1")]
                    ygm = [sbt(st2, "ygm%d" % i, [128, 512], BF16) for i in range(2)]; bygm = [Buf("ygm0"), Buf("ygm1")]
                    for n_i, ci in enumerate(chunks_out):
                        i2 = n_i % 2
                        hsq, hn, s4, bh = hsq_[i2], hn_[i2], s4_[i2], bh_[i2]
                        cs = slice(col(ci), col(ci) + 128)
                        p0 = 2 * i2
                        for kc in range(8):
                            MM(ps[p0][:], inT[:, kc, cs], W[:, kc, 0:512], kc == 0, kc == 7, [bInT[ci]] + bW(0, 512), [PB[p0]])
                        sig_exp(so[i2][:], ps[p0][:], PB[p0], stmp[0][:], bstmp[0], [bso[i2]], False)
                        for kc in range(8):
                            MM(ps[p0 + 1][:], inT[:, kc, cs], W[:, kc, 512:1024], kc == 0, kc == 7, [bInT[ci]] + bW(512, 1024), [PB[p0 + 1]])
                        sig_exp(sgm[i2][:], ps[p0 + 1][:], PB[p0 + 1], stmp[1][:], bstmp[1], [bsgm[i2]], True)
                        h3 = lambda t: t.rearrange("p (h d) -> p h d", d=128)
                        TT(hsq[:], hsum[:, ci, :], hsum[:, ci, :], ALU.mult, [bH[ci]], [bh])
                        RED(s4[:], h3(hsq[:]), ALU.add, [bh], [bh])
                        rsqrt_to(s4[:], s4[:], 1.0 / 128, [bh, bC], [bh])
                        TT(h3(hn[:]), h3(hsum[:, ci, :]), s4[:].unsqueeze(2).broadcast_to([128, 4, 128]), ALU.mult, [bH[ci], bh], [bh])
                        TT(hn[:], hn[:], rowp[:, R_MN:R_MN + 512], ALU.mult, [bh, bRow], [bh])
                        TT(hn[:], hn[:], so[i2][:], ALU.mult, [bh, bso[i2]], [bh])
                        TT(ygm[i2][:], hn[:], sgm[i2][:], ALU.mult, [bh, bsgm[i2]], [bygm[i2]])
                        DMA("sp", yg_d[3, ci * 128:(ci + 1) * 128, :], ygm[i2][:], [bygm[i2]], [bYG[3][ci]])
                    S.emit()
                    if chk("D3"):
                        return

            with contextlib.ExitStack() as st:
                sTt = sbt(st, "sTt", [128, 8, NTOK], BF16); bsTt = [Buf("sTt%d" % c) for c in range(NCH)]
                WmP = sbt(st, "WmP", [128, 8, 512], BF16); WbP = sbt(st, "WbP", [128, 4, 512], BF16); bWmP, bWbP = Buf("WmP"), Buf("WbP")
                WoP = sbt(st, "WoP", [128, 8, 512], BF16); bWoP = Buf("WoP")
                with contextlib.ExitStack() as st1:
                    for p in range(2):
                        with contextlib.ExitStack() as st2:
                            Wm = sbt(st2, "Wm", [128, 8, 4, 512], BF16); bWm = [Buf("Wm%d" % k_) for k_ in range(4)]
                            Wb = sbt(st2, "Wb", [128, 4, 4, 512], BF16); bWb = [Buf("Wb%d" % k_) for k_ in range(4)]
                            src = win_d[l].rearrange("(kc p) n -> p kc n", p=128)
                            for k in range(4):
                                if p == 1 and k == 0:
                                    bWm[0], bWb[0] = bWmP, bWbP
                                    continue
                                c0 = 6672 + k * 1024 + p * 512
                                DMA("pool", Wm[:, :, k, :], src[:, :, c0:c0 + 512], (), [bWm[k]])
                                DMA("pool", Wb[:, k, :, :], wbr_d[l, k].rearrange("(fc p) n -> p fc n", p=128)[:, :, p * 512:(p + 1) * 512], (), [bWb[k]])
                            if p == 0:
                                DMA("pool", WmP[:], src[:, :, 6672 + 512:6672 + 1024], (), [bWmP])
                                DMA("pool", WbP[:], wbr_d[l, 0].rearrange("(fc p) n -> p fc n", p=128)[:, :, 512:1024], (), [bWbP])
                            else:
                                DMA("pool", WoP[:], wout_d[l].rearrange("(kc p) n -> p kc n", p=128)[:, :, 0:512], (), [bWoP])

                            def wm_ap(kc_, k_):
                                return WmP[:, kc_, :] if (p == 1 and k_ == 0) else Wm[:, kc_, k_, :]

                            def wb_ap(k_, fc_):
                                return WbP[:, fc_, :] if (p == 1 and k_ == 0) else Wb[:, k_, fc_, :]
                            ygl = [sbt(st2, "ygl%d" % i, [128, 4, 512], BF16) for i in range(2)]; bygl = [Buf("ygl0"), Buf("ygl1")]
                            ygT = [sbt(st2, "ygT%d" % i, [128, 16, 128], BF16) for i in range(2)]; bygT = [Buf("ygT0"), Buf("ygT1")]
                            sig = [sbt(st2, "sig%d" % i, [128, 512]) for i in range(2)]; bsig = [Buf("sig0"), Buf("sig1")]
                            sacc_ = [sbt(st2, "sacc%d" % i, [128, 512]) for i in range(2)]; tmpm_ = [sbt(st2, "tmpm%d" % i, [128, 512]) for i in range(2)]
                            sbf_ = [sbt(st2, "sbf%d" % i, [128, 512], BF16) for i in range(2)]
                            bsa_ = [Buf("sacc0"), Buf("sacc1")]
                            for n_i, ci in enumerate(chunks_out):
                                i2 = n_i % 2
                                sacc, tmpm, sbf, bsa = sacc_[i2], tmpm_[i2], sbf_[i2], bsa_[i2]
                                cs = slice(col(ci), col(ci) + 128)
                                for k in range(4):
                                    DMA("sp", ygl[i2][:, k, :], yg_d[k, ci * 128:(ci + 1) * 128, :], [bYG[k][ci]], [bygl[i2]])
                                for k in range(4):
                                    for fc in range(4):
                                        idx = k * 4 + fc
                                        bank = 6 + idx // 8
                                        TR(ps_bf(bank)[:, (idx % 8) * 128:(idx % 8 + 1) * 128], ygl[i2][:, k, fc * 128:(fc + 1) * 128], identb[:],
                                           [bygl[i2], bC], [PB[bank]])
                                CP(ygT[i2][:, 0:8, :].rearrange("p a t -> p (a t)"), ps_bf(6)[:, :], [PB[6]], [bygT[i2]])
                                CP(ygT[i2][:, 8:16, :].rearrange("p a t -> p (a t)"), ps_bf(7)[:, :], [PB[7]], [bygT[i2]])
                                for k in range(4):
                                    k2 = k % 2
                                    for fc in range(4):
                                        MM(ps[k2][:], ygT[i2][:, k * 4 + fc, :], wb_ap(k, fc), fc == 0, fc == 3, [bygT[i2], bWb[k]], [PB[k2]])
                                    for kc in range(8):
                                        MM(ps[2 + k2][:], inT[:, kc, cs], wm_ap(kc, k), kc == 0, kc == 7, [bInT[ci], bWm[k]], [PB[2 + k2]])
                                    ACT(sig[k2][:], ps[2 + k2][:], AF.Sigmoid, [PB[2 + k2]], [bsig[k2]])
                                    if k == 0:
                                        TT(sacc[:], ps[k2][:], sig[k2][:], ALU.mult, [PB[k2], bsig[k2]], [bsa])
                                    else:
                                        TT(tmpm[:], ps[k2][:], sig[k2][:], ALU.mult, [PB[k2], bsig[k2]], [bsa])
                                        TT(sacc[:], sacc[:], tmpm[:], ALU.add, [bsa], [bsa])
                                CP(sbf[:], sacc[:], [bsa], [bsa])
                                for dc in range(4):
                                    TR(ps_bf(4 + i2)[:, dc * 128:(dc + 1) * 128], sbf[:, dc * 128:(dc + 1) * 128], identb[:], [bsa, bC], [PB[4 + i2]])
                                CP(sTt[:, p * 4:(p + 1) * 4, ci * 128:(ci + 1) * 128], ps_bf(4 + i2)[:, 0:512].rearrange("p (a t) -> p a t", a=4),
                                   [PB[4 + i2]], [bsTt[ci]])
                            S.emit()
                            if chk("E%d" % p):
                                return
                with contextlib.ExitStack() as st2:
                    Wo = sbt(st2, "Wo", [128, 8, D], BF16); bWo = [Buf("Wo0"), Buf("Wo1")]
                    bWo[0] = bWoP
                    for hf_ in range(1, 2):
                        DMA("pool", Wo[:, :, hf_ * 512:(hf_ + 1) * 512], wout_d[l].rearrange("(kc p) n -> p kc n", p=128)[:, :, hf_ * 512:(hf_ + 1) * 512], (), [bWo[hf_]])
                    yf = [sbt(st2, "yf%d" % i, [128, D]) for i in range(2)]; byf = [Buf("yf0"), Buf("yf1")]
                    rs = [sbt(st2, "rs%d" % i, [128, D]) for i in range(2)]; brs = [Buf("rs0"), Buf("rs1")]
                    ob_ = [sbt(st2, "ob%d" % i, [128, D]) for i in range(2)]; bob = [Buf("ob0"), Buf("ob1")]
                    junk = sbt(st2, "junkf", [128, D], BF16); bj = Buf("junkf")
                    ss = [sbt(st2, "ssf%d" % i, [128, 1]) for i in range(2)]; bss = [Buf("ssf0"), Buf("ssf1")]
                    for n_i, ci in enumerate(chunks_out):
                        i2 = n_i % 2
                        v = 1 if ci < 2 else 0
                        tk = slice(ci * 128, (ci + 1) * 128)
                        DMA("sp", rs[i2][:], src_d[tk, :], [bRes[l][ci]], [brs[i2]])
                        for half in range(2):
                            bank = 2 * i2 + half
                            for dc in range(8):
                                MM(ps[bank][:], sTt[:, dc, tk], (WoP[:, dc, :] if half == 0 else Wo[:, dc, 512:1024]), dc == 0, dc == 7, [bsTt[ci], bWo[half]], [PB[bank]])
                            ACP(yf[i2][:, half * 512:(half + 1) * 512], ps[bank][:], [PB[bank]], [byf[i2]])
                        MSET(ss[i2][:], 0.0, [bss[i2]])
                        ACT(junk[:], yf[i2][:], AF.Square, [byf[i2], bss[i2]], [bj, bss[i2]], accum=ss[i2][:])
                        rsqrt_to(ss[i2][:], ss[i2][:], 1.0 / D, [bss[i2], bC], [bss[i2]])
                        STT(yf[i2][:], yf[i2][:], ss[i2][:, 0:1], gprow[v][:], ALU.mult, ALU.mult, [byf[i2], bss[i2], bGp[v]], [byf[i2]])
                        TT(ob_[i2][:], yf[i2][:], rs[i2][:], ALU.add, [byf[i2], brs[i2]], [bob[i2]])
                        if last:
                            DMA("sp", out_d[(ci - 2) * 128:(ci - 1) * 128, :], ob_[i2][:], [bob[i2]], [bRes[l + 1][ci]])
                        else:
                            DMA("sp", r1_d[tk, :], ob_[i2][:], [bob[i2]], [bRes[l + 1][ci]])
                    S.emit()
                    if chk("F"):
                        return
        try:
            for l_ in range(depth):
                if not stopflag[0]:
                    layer(l_)
        except _Stop:
            pass
        S.finish()
        stats = dict(n_ops=len(S.ops), n_wait=S.nwait)
    return nc, stats


def _fm(v, n):
    return np.ascontiguousarray(np.asarray(v, np.float32).reshape(n, 128).T)


def make_inputs(inputs):
    f = lambda k: np.asarray(inputs[k], np.float32)
    x, c, ctx, c_ctx = f("x"), f("c"), f("ctx"), f("c_ctx")
    L = 2
    shared = {}
    shared["w_mod"] = np.ascontiguousarray(f("w_mod"))
    shared["b_modT"] = np.stack([_fm(f("b_mod")[l], 24) for l in range(L)])
    shared["g_preT"] = np.stack([_fm(f("g_pre")[l], 8) for l in range(L)])
    shared["g_postT"] = np.stack([_fm(f("g_post")[l], 8) for l in range(L)])
    shared["w_in"] = np.ascontiguousarray(f("w_in"))
    rows = []
    for l in range(L):
        r = np.concatenate([f("a_q_norm")[l], f("a_k_norm")[l], f("w_sink")[l], f("sg_ln_g")[l], f("sg_ln_b")[l], f("m_norm")[l],
                            f("m_b_i")[l].reshape(-1), f("m_b_f")[l].reshape(-1)])
        assert r.shape[0] == NROW
        rows.append(np.broadcast_to(r[None, :], (128, NROW)))
    shared["rowp"] = np.ascontiguousarray(np.stack(rows))
    shared["convT"] = np.ascontiguousarray(f("m_conv").reshape(L, 3, 12, 128).transpose(0, 3, 2, 1))
    shared["sg_wT"] = np.ascontiguousarray(f("sg_w").transpose(0, 3, 1, 2))
    shared["sg_bT"] = np.ascontiguousarray(f("sg_b").transpose(0, 2, 1))
    shared["w_branch"] = np.ascontiguousarray(f("w_branch"))
    shared["w_out"] = np.ascontiguousarray(f("w_out"))
    shared["ident"] = np.eye(128, dtype=np.float32)
    s_ = np.arange(128)
    shared["triF"] = (s_[:, None] <= s_[None, :]).astype(np.float32)
    shared["triB"] = (s_[:, None] >= s_[None, :]).astype(np.float32)
    t = np.arange(2048)
    row = (t // 64).astype(np.float32); cl = (t % 64).astype(np.float32)
    inv = (np.float32(10000.0) ** (-np.arange(16, dtype=np.float32) / np.float32(16))).astype(np.float32)
    ang = np.concatenate([row[:, None] * inv, cl[:, None] * inv], axis=-1).astype(np.float32)
    shared["ropec"] = np.ascontiguousarray(np.cos(ang).astype(np.float32).reshape(16, 128, 32).transpose(1, 0, 2))
    shared["ropes"] = np.ascontiguousarray(np.sin(ang).astype(np.float32).reshape(16, 128, 32).transpose(1, 0, 2))
    in_maps = []
    for b in range(x.shape[0]):
        m = dict(shared)
        m["xin"] = np.ascontiguousarray(np.concatenate([ctx[b], x[b]], axis=0))
        m["cc"] = np.ascontiguousarray(np.stack([_fm(c[b], 8), _fm(c_ctx, 8)], axis=-1))
        in_maps.append(m)
    return in_maps


_CACHE = {}


def kernel(**inputs):
    in_maps = make_inputs(inputs)
    if "nc" not in _CACHE:
        _CACHE["nc"] = build()[0]
    nc = _CACHE["nc"]
    res = run_bass_kernel_spmd(nc, in_maps, core_ids=list(range(len(in_maps))))
    out = np.stack([np.asarray(r["out"], np.float32) for r in res.results], axis=0)
    return out
```

```python
import contextlib
import numpy as np
import ml_dtypes
import concourse.bass as bass
import concourse.mybir as mybir
from concourse.bass_utils import run_bass_kernel_spmd

F32 = mybir.dt.float32
BF16 = mybir.dt.bfloat16
AF = mybir.ActivationFunctionType
ALU = mybir.AluOpType
AX = mybir.AxisListType

COMPUTE = ("pe", "act", "dve", "pool")
ENGMAP = {"pe": "tensor", "act": "scalar", "dve": "vector", "pool": "gpsimd", "sp": "sync"}


class Buf:
    __slots__ = ("name", "excl")

    def __init__(self, name, excl=False):
        self.name = name
        self.excl = excl

    def __repr__(self):
        return "Buf(%s)" % self.name


class Sched:
    def __init__(self, nc, sems):
        self.nc = nc
        self.sems = sems
        self.ops = []
        self.cost = []
        self.window = 4096
        self.strict = STRICT_SYNC
        self.done = 0
        self.last_w = {}
        self.readers = {}
        self.val = []
        self.cnt = {e: 0 for e in COMPUTE}
        self.dma_rr = {}
        self.dma_cnt = {}
        self.known = {}
        self.finals = {}
        self.nwait = 0

    def add(self, eng, fn, reads=(), writes=(), dma=False, cost=0.3):
        self.ops.append((eng, fn, tuple(reads), tuple(writes), dma))
        self.cost.append(cost)

    def pe(self, fn, r=(), w=(), cost=0.25):
        self.add("pe", fn, r, w, cost=cost)

    def act(self, fn, r=(), w=(), cost=0.6):
        self.add("act", fn, r, w, cost=cost)

    def dve(self, fn, r=(), w=(), cost=0.3):
        self.add("dve", fn, r, w, cost=cost)

    def dma(self, eng, fn, r=(), w=(), cost=4.0):
        self.add(eng, fn, r, w, dma=True, cost=cost)

    def list_schedule(self, lo, n, alld, streams, window):
        ops, cost = self.ops, self.cost
        LAT = 0.7
        dependents = {}
        nun = {}
        ready = {}
        for i in range(lo, n):
            nun[i] = len(alld[i])
            ready[i] = 0.0
            for j in alld[i]:
                dependents.setdefault(j, []).append(i)
        pend = {e: list(v) for e, v in streams.items()}
        free = {e: 0.0 for e in streams}
        order = {e: [] for e in streams}
        cache = {}
        remaining = n - lo
        while remaining:
            best = None
            for e, pl in pend.items():
                if not pl:
                    continue
                c = cache.get(e)
                if c is None:
                    fe = free[e]
                    c = False
                    for pos in range(min(window, len(pl))):
                        i = pl[pos]
                        if nun[i]:
                            continue
                        st = ready[i] if ready[i] > fe else fe
                        if c is False or st < c[0]:
                            c = (st, i, pos)
                            if st <= fe:
                                break
                    cache[e] = c
                if c is not False and (best is None or (c[0], c[1]) < (best[1][0], best[1][1])):
                    best = (e, c)
            assert best is not None, "list scheduler deadlock"
            e, (st, i, pos) = best
            del pend[e][pos]
            order[e].append(i)
            dma = ops[i][4]
            fin = st + cost[i]
            free[e] = st + (0.06 if dma else cost[i])
            cache[e] = None
            for k in dependents.get(i, ()):
                nun[k] -= 1
                t = fin + (LAT if (ops[k][0] != e or dma) else 0.05)
                if t > ready[k]:
                    ready[k] = t
                cache[ops[k][0]] = None
            remaining -= 1
        self.sim_time = max(free.values())
        return order

    def emit(self, barrier=True):
        ops = self.ops
        lo, n = self.done, len(self.ops)
        if lo == n:
            return
        last_w, readers = self.last_w, self.readers
        deps = {}
        alld = {}
        for i in range(lo, n):
            eng, fn, reads, writes, dma = ops[i]
            d = set()
            exw = list(writes) + [r for r in reads if r.excl]
            for r in reads:
                if r.excl:
                    continue
                lw = last_w.get(r)
                if lw is not None:
                    d.add(lw)
            for w in exw:
                lw = last_w.get(w)
                if lw is not None:
                    d.add(lw)
                for rr in readers.get(w, ()):
                    d.add(rr)
            for r in reads:
                if not r.excl:
                    readers.setdefault(r, []).append(i)
            for w in exw:
                last_w[w] = i
                readers[w] = []
            d.discard(i)
            alld[i] = set(j for j in d if j >= lo)
            dd = set()
            for j in d:
                if j < lo and barrier:
                    continue
                ej, _, rj, wj, dmaj = ops[j]
                if (not dma) and (not dmaj) and ej == eng:
                    if eng == "pe":
                        continue
                    if not self.strict:
                        wset = set(wj) | set(r for r in rj if r.excl)
                        if not (wset & set(reads)):
                            continue
                dd.add(j)
            deps[i] = dd
        signal = set()
        for i in range(lo, n):
            signal |= deps[i]
        streams = {}
        for i in range(lo, n):
            streams.setdefault(ops[i][0], []).append(i)
        if self.window > 1:
            streams = self.list_schedule(lo, n, alld, streams, self.window)
        for eng, idxs in streams.items():
            for i in reversed(idxs):
                if not ops[i][4]:
                    signal.add(i)
                    break
        self.val.extend([None] * (n - lo))
        val = self.val
        dma_prev = {}
        issue = []
        for eng, idxs in streams.items():
            issue.extend(idxs)
        for i in issue:
            eng, fn, reads, writes, dma = ops[i]
            if dma:
                pool = self.sems["dma"][eng]
                k = self.dma_rr.get(eng, 0)
                self.dma_rr[eng] = (k + 1) % len(pool)
                s = pool[k]
                c = self.dma_cnt.get((eng, k), 0)
                if c > 0:
                    dma_prev[i] = (s, 16 * c)
                self.dma_cnt[(eng, k)] = c + 1
                val[i] = (s, 16 * (c + 1))
            elif i in signal:
                self.cnt[eng] += 1
                val[i] = (self.sems[eng], self.cnt[eng])
        prev_finals = dict(self.finals)

        def make_stream(eng, idxs):
            def body(e):
                known = self.known.setdefault(eng, {})

                def wait(s, v):
                    if known.get(s, 0) < v:
                        e.wait_ge(s, v)
                        known[s] = v
                        self.nwait += 1
                if barrier:
                    for s, v in prev_finals.items():
                        wait(s, v)
                for i in idxs:
                    _, fn, reads, writes, dma = ops[i]
                    need = {}
                    for j in deps[i]:
                        s, v = val[j]
                        if need.get(s, 0) < v:
                            need[s] = v
                    if i in dma_prev:
                        s, v = dma_prev[i]
                        if need.get(s, 0) < v:
                            need[s] = v
                    for s, v in need.items():
                        wait(s, v)
                    ins = fn(e)
                    if val[i] is not None:
                        ins.then_inc(val[i][0], 16 if dma else 1)
            return body

        with self.nc.Block() as block:
            for eng, idxs in streams.items():
                getattr(block, ENGMAP[eng])(make_stream(eng, idxs))
        for i in range(lo, n):
            if val[i] is not None:
                s, v = val[i]
                if self.finals.get(s, 0) < v:
                    self.finals[s] = v
        self.done = n

    def finish(self):
        finals = dict(self.finals)

        def body(e):
            for s, v in finals.items():
                e.wait_ge(s, v)
        with self.nc.Block() as block:
            block.sync(body)


STRICT_SYNC = True
D = 1024
NCH = 18
NTOK = NCH * 128
TW = 2308
D_IN = 10768
EPS = 1e-6
R_QN, R_KN, R_SINK, R_LNG, R_LNB, R_MN, R_BI, R_BF, R_CONV = 0, 64, 128, 136, 648, 1160, 1672, 1680, 1688
NROW = 1688


def col(ci):
    return 128 * ci + (1 if ci < 2 else 3)


class _Stop(Exception):
    pass


def build(depth=2, taps=(), stop=None):
    nc = bass.Bass("TRN2", target_bir_lowering=False)

    def din(name, shape, dt=F32):
        return nc.dram_tensor(name, list(shape), dt, kind="ExternalInput").ap()
    xin = din("xin", [NTOK, D])
    cc_d = din("cc", [128, 8, 2])
    wmod_d = din("w_mod", [2, D, 3 * D])
    bmodT_d = din("b_modT", [2, 128, 24])
    gpreT_d = din("g_preT", [2, 128, 8])
    gpostT_d = din("g_postT", [2, 128, 8])
    win_d = din("w_in", [2, D, D_IN])
    rowp_d = din("rowp", [2, 128, NROW])
    convT_d = din("convT", [2, 128, 12, 3])
    sgwT_d = din("sg_wT", [2, 128, 4, 128])
    sgbT_d = din("sg_bT", [2, 128, 4])
    wbr_d = din("w_branch", [2, 4, 512, D])
    wout_d = din("w_out", [2, D, D])
    ident_d = din("ident", [128, 128])
    triF_d = din("triF", [128, 128])
    triB_d = din("triB", [128, 128])
    cos_d = din("ropec", [128, 16, 32])
    sin_d = din("ropes", [128, 16, 32])
    out_d = nc.dram_tensor("out", [2048, D], F32, kind="ExternalOutput").ap()
    dbg_kind = "ExternalOutput" if taps else "Internal"
    r1_d = nc.dram_tensor("r1", [NTOK, D], F32, kind=dbg_kind).ap()
    yg_d = nc.dram_tensor("ygs", [4, NTOK, 512], BF16, kind=dbg_kind).ap()
    tap_d = {}
    for nm, shp in taps:
        tap_d[nm] = nc.dram_tensor(nm, list(shp), F32, kind="ExternalOutput").ap()

    with contextlib.ExitStack() as gst:
        sems = {k: gst.enter_context(nc.semaphore("s_" + k)) for k in COMPUTE}
        sems["dma"] = {"sp": [gst.enter_context(nc.semaphore("d_sp%d" % i)) for i in range(8)],
                       "pool": [gst.enter_context(nc.semaphore("d_pl%d" % i)) for i in range(8)]}
        S = Sched(nc, sems)

        uniq = [0]

        def sbt(st, name, shape, dt=F32):
            uniq[0] += 1
            return st.enter_context(nc.sbuf_tensor("%s_%d" % (name, uniq[0]), list(shape), dt))

        def fsz(ap):
            n_ = 1
            for d_ in ap.shape[1:]:
                n_ *= d_
            return n_

        def MM(out, lhsT, rhs, start, stop, r, w):
            c_ = (64 + fsz(rhs)) / 2400.0 * (4.0 if lhsT.dtype == F32 else 1.0)
            S.pe(lambda e: e.matmul(out, lhsT=lhsT, rhs=rhs, start=start, stop=stop), r, w, cost=c_)

        def TR(out, in_, idn, r, w):
            S.pe(lambda e: e.transpose(out, in_, idn), r, w, cost=(0.3 if in_.dtype == F32 else 0.1))

        def ACT(out, in_, func, r, w, bias=None, scale=None, accum=None):
            kw = {}
            if bias is not None:
                kw["bias"] = bias
            if scale is not None:
                kw["scale"] = scale
            if accum is not None:
                kw["accum_out"] = accum
            S.act(lambda e: e.activation(out=out, in_=in_, func=func, **kw), r, w, cost=(224 + fsz(out)) / 1200.0)

        def TS(out, in0, s1, s2, op0, op1, r, w):
            c_ = (70 + fsz(out)) / 960.0
            if s2 is None:
                S.dve(lambda e: e.tensor_scalar(out=out, in0=in0, scalar1=s1, scalar2=None, op0=op0), r, w, cost=c_)
            else:
                S.dve(lambda e: e.tensor_scalar(out=out, in0=in0, scalar1=s1, scalar2=s2, op0=op0, op1=op1), r, w, cost=c_)

        def TT(out, in0, in1, op, r, w):
            S.dve(lambda e: e.tensor_tensor(out=out, in0=in0, in1=in1, op=op), r, w, cost=(70 + fsz(out)) / 960.0)

        def STT(out, in0, scalar, in1, op0, op1, r, w):
            S.dve(lambda e: e.scalar_tensor_tensor(out=out, in0=in0, scalar=scalar, in1=in1, op0=op0, op1=op1), r, w,
                  cost=(70 + fsz(out)) / 960.0)

        def CP(out, in_, r, w):
            S.dve(lambda e: e.tensor_copy(out=out, in_=in_), r, w, cost=(70 + fsz(out) * (0.5 if out.dtype == BF16 else 1.0)) / 960.0)

        def ACP(out, in_, r, w):
            S.act(lambda e: e.copy(out=out, in_=in_), r, w, cost=(224 + fsz(out)) / 1200.0)

        def RED(out, in_, op, r, w):
            S.dve(lambda e: e.tensor_reduce(out=out, in_=in_, axis=AX.X, op=op), r, w, cost=(70 + fsz(in_)) / 960.0)

        def RCP(out, in_, r, w):
            S.dve(lambda e: e.reciprocal(out=out, in_=in_), r, w, cost=(70 + fsz(out)) / 960.0)

        def PTS(out, in0, s1, r, w):
            S.add("pool", lambda e: e.tensor_scalar(out=out, in0=in0, scalar1=s1, scalar2=None, op0=ALU.mult), r, w, cost=(150 + fsz(out)) / 1000.0)

        def PTT(out, in0, in1, op, r, w):
            S.add("pool", lambda e: e.tensor_tensor(out=out, in0=in0, in1=in1, op=op), r, w, cost=(150 + fsz(out)) / 1000.0)

        def PSTT(out, in0, scalar, in1, op0, op1, r, w):
            S.add("pool", lambda e: e.scalar_tensor_tensor(out=out, in0=in0, scalar=scalar, in1=in1, op0=op0, op1=op1), r, w,
                  cost=(150 + fsz(out)) / 1000.0)

        def MSET(ap, v, w):
            S.dve(lambda e: e.memset(ap, v), (), w, cost=(70 + fsz(ap) * 0.5) / 960.0)

        def DMA(q, out, in_, r, w):
            nb = fsz(out) * 128 * (2 if out.dtype == BF16 else 4)
            S.dma(q, lambda e: e.dma_start(out=out, in_=in_), r, w, cost=2.0 + nb / 150e3)

        def rsqrt_to(out, in_, scale, r, w):
            ACT(out, in_, AF.Ln, r, w, bias=epsc[:, 0:1], scale=scale)
            ACT(out, out, AF.Exp, w, w, scale=-0.5)

        def sig_exp(out, psrc, pb, tmp, btmp, w, mul_x):
            ACT(tmp, psrc, AF.Exp, [pb], [btmp], scale=-1.0)
            ACT(tmp, tmp, AF.Ln, [btmp, bC], [btmp], bias=onec[:, 0:1])
            if mul_x:
                ACT(tmp, tmp, AF.Exp, [btmp], [btmp], scale=-1.0)
                TT(out, psrc, tmp, ALU.mult, [pb, btmp], w)
            else:
                ACT(out, tmp, AF.Exp, [btmp], w, scale=-1.0)

        psw = [gst.enter_context(nc.psum_tensor("psw%d" % i, [128, 1024], F32)) for i in range(4)]
        ps = [psw[i // 2][:, (i % 2) * 512:(i % 2 + 1) * 512] for i in range(8)]
        PB = [Buf("ps%d" % i, excl=True) for i in range(8)]
        ident = sbt(gst, "ident", [128, 128]); identb = sbt(gst, "identb", [128, 128], BF16)
        triF = sbt(gst, "triF", [128, 128]); triB = sbt(gst, "triB", [128, 128])
        triFb = sbt(gst, "triFb", [128, 128], BF16); triBb = sbt(gst, "triBb", [128, 128], BF16)
        ones = sbt(gst, "ones", [128, 128]); epsc = sbt(gst, "epsc", [128, 1]); onec = sbt(gst, "onec", [128, 1])
        inT = sbt(gst, "inT", [128, 8, TW], BF16)
        rowp = sbt(gst, "rowp", [128, NROW])
        modT = sbt(gst, "modT", [128, 24, 2]); aT = sbt(gst, "aT", [128, 8, 2]); gpT = sbt(gst, "gpT", [128, 8, 2])
        gprow = [sbt(gst, "gprow%d" % v, [128, D]) for v in range(2)]
        bC = Buf("consts"); bInT = [Buf("inT%d" % i) for i in range(NCH)]; bHalo = Buf("halo")
        bRow = Buf("rowp"); bMod = Buf("mod"); bGp = [Buf("gprow0"), Buf("gprow1")]
        bYG = [[Buf("yg%d_%d" % (k, ci)) for ci in range(NCH)] for k in range(4)]
        bRes = [[Buf("res%d_%d" % (l, ci)) for ci in range(NCH)] for l in range(3)]

        def ps_bf(i):
            return ps[i].bitcast(BF16)

        DMA("sp", ident[:], ident_d, (), [bC]); DMA("sp", triF[:], triF_d, (), [bC]); DMA("sp", triB[:], triB_d, (), [bC])
        CP(identb[:], ident[:], [bC], [bC]); CP(triFb[:], triF[:], [bC], [bC]); CP(triBb[:], triB[:], [bC], [bC])
        MSET(ones[:], 1.0, [bC]); MSET(epsc[:], EPS, [bC]); MSET(onec[:], 1.0, [bC])
        S.dve(lambda e: e.memset(inT[:], 0.0), (), [bHalo] + bInT)
        S.emit()

        def load_w(st, name, l, c0, n, q="pool", bounds=None, step=512):
            t = sbt(st, name, [128, 8, n], BF16)
            src = win_d[l].rearrange("(kc p) n -> p kc n", p=128)
            if bounds is None:
                bounds = [(a, min(a + step, n)) for a in range(0, n, step)]
            bufs = [Buf("%s_%d" % (name, i)) for i in range(len(bounds))]
            for i, (a, b) in enumerate(bounds):
                DMA(q, t[:, :, a:b], src[:, :, c0 + a:c0 + b], (), [bufs[i]])

            def wb(a, b):
                return [bufs[i] for i, (x, y) in enumerate(bounds) if x < b and y > a]
            return t, wb

        stopflag = [False]

        def chk(name):
            if stop == name:
                stopflag[0] = True
            return stopflag[0]

        def layer(l):
            last = (l == depth - 1)
            src_d = xin if l == 0 else r1_d
            chunks_out = list(range(2, NCH)) if last else list(range(NCH))
            stM = contextlib.ExitStack()
            if True:
                st = stM
                cc = sbt(st, "cc", [128, 8, 2]); sc = sbt(st, "sc", [128, 8, 2])
                bmT = sbt(st, "bmT", [128, 24]); gpre = sbt(st, "gpre", [128, 8]); gpost = sbt(st, "gpost", [128, 8])
                wm = [sbt(st, "wm%d" % i, [128, 8, 512], BF16) for i in range(3)]
                scb = sbt(st, "scb", [128, 8, 2], BF16)
                diag = [sbt(st, "diag%d" % i, [128, 128]) for i in range(2)]
                tmp82 = sbt(st, "tmp82", [128, 8, 2])
                bcc, bsc, bsm = Buf("cc"), Buf("sc"), Buf("small")
                bwm = [Buf("wm0"), Buf("wm1"), Buf("wm2")]; bdg = [Buf("dg0"), Buf("dg1")]
                DMA("sp", cc[:], cc_d, (), [bcc]); DMA("sp", bmT[:], bmodT_d[l], (), [bsm])
                DMA("sp", gpre[:], gpreT_d[l], (), [bsm]); DMA("sp", gpost[:], gpostT_d[l], (), [bsm])
                DMA("sp", rowp[:], rowp_d[l], (), [bRow])
                ACT(sc[:], cc[:], AF.Silu, [bcc], [bsc])
                CP(scb[:], sc[:], [bsc], [bsc])
                wsrc = wmod_d[l].rearrange("(kc p) n -> p kc n", p=128)
                for j in range(6):
                    DMA("pool", wm[j % 3][:], wsrc[:, :, j * 512:(j + 1) * 512], (), [bwm[j % 3]])
                    for sub in range(4):
                        cidx = j * 4 + sub
                        for kc in range(8):
                            MM(ps[0][:, cidx * 2:cidx * 2 + 2], wm[j % 3][:, kc, sub * 128:(sub + 1) * 128], scb[:, kc, :],
                               kc == 0, kc == 7, [bwm[j % 3], bsc], [PB[0]])
                TT(modT[:], ps[0][:, 0:48].rearrange("p (c v) -> p c v", v=2), bmT[:].unsqueeze(2).broadcast_to([128, 24, 2]),
                   ALU.add, [PB[0], bsm], [bMod])
                TS(tmp82[:], modT[:, 8:16, :], 1.0, None, ALU.add, None, [bMod], [bsm])
                TT(aT[:], tmp82[:], gpre[:].unsqueeze(2).broadcast_to([128, 8, 2]), ALU.mult, [bsm], [bMod])
                TT(gpT[:], modT[:, 16:24, :], gpost[:].unsqueeze(2).broadcast_to([128, 8, 2]), ALU.mult, [bMod, bsm], [bMod])
                for v in range(2):
                    for fc in range(8):
                        dg = diag[fc % 2]
                        TS(dg[:], ident[:], gpT[:, fc, v:v + 1], None, ALU.mult, None, [bMod, bC], [bdg[fc % 2]])
                        bank = 1 + fc // 4
                        MM(ps[bank][:, (fc % 4) * 128:(fc % 4 + 1) * 128], ones[:], dg[:], True, True, [bC, bdg[fc % 2]], [PB[bank]])
                    ACP(gprow[v][:, 0:512], ps[1][:], [PB[1]], [bGp[v]])
                    ACP(gprow[v][:, 512:1024], ps[2][:], [PB[2]], [bGp[v]])
            with contextlib.ExitStack() as st:
                xc = [sbt(st, "xc%d" % i, [128, D]) for i in range(2)]
                xs = [sbt(st, "xs%d" % i, [128, D]) for i in range(2)]
                junk = sbt(st, "junk", [128, D], BF16)
                ss = [sbt(st, "ss%d" % i, [128, 1]) for i in range(2)]
                bxc = [Buf("xc0"), Buf("xc1")]; bxs = [Buf("xs0"), Buf("xs1")]; bj = Buf("junk"); bss = [Buf("ss0"), Buf("ss1")]
                for ci in range(NCH):
                    i2 = ci % 2
                    v = 1 if ci < 2 else 0
                    DMA("sp", xc[i2][:], src_d[ci * 128:(ci + 1) * 128, :], [bRes[l][ci]], [bxc[i2]])
                    MSET(ss[i2][:], 0.0, [bss[i2]])
                    ACT(junk[:], xc[i2][:], AF.Square, [bxc[i2], bss[i2]], [bj, bss[i2]], accum=ss[i2][:])
                    rsqrt_to(ss[i2][:], ss[i2][:], 1.0 / D, [bss[i2], bC], [bss[i2]])
                    TS(xs[i2][:], xc[i2][:], ss[i2][:, 0:1], None, ALU.mult, None, [bxc[i2], bss[i2]], [bxs[i2]])
                    for fc in range(8):
                        bank = 3 + fc // 4 + 2 * i2
                        TR(ps[bank][:, (fc % 4) * 128:(fc % 4 + 1) * 128], xs[i2][:, fc * 128:(fc + 1) * 128], ident[:],
                           [bxs[i2], bC], [PB[bank]])
                    for fc in range(8):
                        bank = 3 + fc // 4 + 2 * i2
                        if fc % 2 == 0:
                            TS(inT[:, fc, col(ci):col(ci) + 128], ps[bank][:, (fc % 4) * 128:(fc % 4 + 1) * 128],
                               aT[:, fc, v:v + 1], modT[:, fc, v:v + 1], ALU.mult, ALU.add, [PB[bank], bMod], [bInT[ci]])
                        else:
                            ACT(inT[:, fc, col(ci):col(ci) + 128], ps[bank][:, (fc % 4) * 128:(fc % 4 + 1) * 128], AF.Identity,
                                [PB[bank], bMod], [bInT[ci]], bias=modT[:, fc, v:v + 1], scale=aT[:, fc, v:v + 1])
                S.emit()
            stM.close()
            if "inT" in tap_d and l == 0:
                with contextlib.ExitStack() as st:
                    tt = sbt(st, "tapt", [128, 8, TW]); bt = Buf("tapt")
                    CP(tt[:], inT[:], bInT + [bHalo], [bt])
                    DMA("sp", tap_d["inT"], tt[:], [bt], ())
                    S.emit()
            if chk("N"):
                return

            def phase_B(st, BB):
                W, bW = load_w(st, "WB", l, 2560, 1536)
                swf = sbt(st, "swf", [128, 4, 128]); swb = sbt(st, "swb", [128, 4, 128], BF16); sbT = sbt(st, "sbT", [128, 4])
                bsw = Buf("sgw")
                NB_ = 3
                ug = [sbt(st, "ug%d" % i, [128, 512]) for i in range(NB_)]; bug = [Buf("ug%d" % i) for i in range(NB_)]
                vg_ = [sbt(st, "vg%d" % i, [128, 512]) for i in range(NB_)]; xcn_ = [sbt(st, "xcn%d" % i, [128, 512]) for i in range(NB_)]
                vn_ = [sbt(st, "vn%d" % i, [128, 512], BF16) for i in range(NB_)]
                junk = sbt(st, "junkb", [128, 512], BF16)
                sv_ = [sbt(st, "sv%d" % i, [128, 2]) for i in range(NB_)]; bv_ = [Buf("vwork%d" % i) for i in range(NB_)]
                sgt = [sbt(st, "sgt%d" % i, [128, 512]) for i in range(NB_)]; bsgt = [Buf("sgt%d" % i) for i in range(NB_)]
                mx_ = [sbt(st, "mx%d" % i, [128, 512]) for i in range(NB_)]; ygb = [sbt(st, "ygb%d" % i, [128, 512], BF16) for i in range(NB_)]
                bmx_ = [Buf("mx%d" % i) for i in range(NB_)]; bygb = [Buf("ygb%d" % i) for i in range(NB_)]
                bjb = Buf("junkb")
                DMA("sp", swf[:], sgwT_d[l], (), [bsw]); DMA("sp", sbT[:], sgbT_d[l], (), [bsw])
                CP(swb[:], swf[:], [bsw], [bsw])
                for n_i, ci in enumerate(chunks_out):
                    i2 = n_i % NB_
                    pb0 = 0
                    vg, xcn, vn, sv, bv, mx, bmx = vg_[i2], xcn_[i2], vn_[i2], sv_[i2], bv_[i2], mx_[i2], bmx_[i2]
                    cs = slice(col(ci), col(ci) + 128)
                    for kc in range(8):
                        MM(ps[BB[0]][:], inT[:, kc, cs], W[:, kc, 0:512], kc == 0, kc == 7, [bInT[ci]] + bW(0, 512), [PB[BB[0]]])
                    ACT(ug[i2][:], ps[BB[0]][:], AF.Gelu_apprx_tanh, [PB[BB[0]]], [bug[i2]])
                    for kc in range(8):
                        MM(ps[BB[1]][:], inT[:, kc, cs], W[:, kc, 512:1024], kc == 0, kc == 7, [bInT[ci]] + bW(512, 1024), [PB[BB[1]]])
                    MSET(sv[:], 0.0, [bv])
                    ACT(vg[:], ps[BB[1]][:], AF.Gelu_apprx_tanh, [PB[BB[1]], bv], [bv], accum=sv[:, 0:1])
                    TS(sv[:, 0:1], sv[:, 0:1], -1.0 / 512, None, ALU.mult, None, [bv], [bv])
                    TS(xcn[:], vg[:], sv[:, 0:1], None, ALU.add, None, [bv], [bv])
                    ACT(junk[:], xcn[:], AF.Square, [bv], [bv, bjb], accum=sv[:, 1:2])
                    rsqrt_to(sv[:, 1:2], sv[:, 1:2], 1.0 / 512, [bv, bC], [bv])
                    STT(xcn[:], xcn[:], sv[:, 1:2], rowp[:, R_LNG:R_LNG + 512], ALU.mult, ALU.mult, [bv, bRow], [bv])
                    TT(vn[:], xcn[:], rowp[:, R_LNB:R_LNB + 512], ALU.add, [bv, bRow], [bv])
                    for g in range(4):
                        MM(ps[BB[2]][:, g * 128:(g + 1) * 128], swb[:, g, :], vn[:, g * 128:(g + 1) * 128], True, True, [bsw, bv], [PB[BB[2]]])
                    for kc in range(8):
                        MM(ps[BB[3]][:], inT[:, kc, cs], W[:, kc, 1024:1536], kc == 0, kc == 7, [bInT[ci]] + bW(1024, 1536), [PB[BB[3]]])
                    sig_exp(sgt[i2][:], ps[BB[3]][:], PB[BB[3]], sgt[i2][:], bsgt[i2], [bsgt[i2]], True)
                    TT(mx[:].rearrange("p (g c) -> p g c", g=4), ps[BB[2]][:].rearrange("p (g c) -> p g c", g=4),
                       sbT[:].unsqueeze(2).broadcast_to([128, 4, 128]), ALU.add, [PB[BB[2]], bsw], [bmx])
                    TT(mx[:], mx[:], ug[i2][:], ALU.mult, [bmx, bug[i2]], [bmx])
                    TT(ygb[i2][:], mx[:], sgt[i2][:], ALU.mult, [bmx, bsgt[i2]], [bygb[i2]])
                    DMA("sp", yg_d[2, ci * 128:(ci + 1) * 128, :], ygb[i2][:], [bygb[i2]], [bYG[2][ci]])

            for br, c0 in ((0, 0), (1, 1280)):
                is_win = (br == 1)
                with contextlib.ExitStack() as st:
                    W, bW = load_w(st, "WA", l, c0, 1280, bounds=[(512, 768), (0, 512), (768, 1280)])
                    ropec = sbt(st, "ropec", [128, 16, 32]); ropes = sbt(st, "ropes", [128, 16, 32]); bRp = Buf("rope")
                    DMA("sp", ropec[:], cos_d, (), [bRp]); DMA("sp", ropes[:], sin_d, (), [bRp])
                    kT = sbt(st, "kT", [128, 2, NTOK], BF16); bkT = Buf("kT")
                    vaug = sbt(st, "vaug", [128, NCH, 2, 68], BF16); bV = Buf("vaug")
                    PT = [sbt(st, "PT%d" % i, [128, 6 if is_win else 18, 512], BF16) for i in range(2)]; bPT = [Buf("PT0"), Buf("PT1")]
                    NQB = 2 if is_win else 3
                    STW = [1] if is_win else [1, 2]
                    BK_PVT, BK_RES = (4, 5) if is_win else (6, 7)
                    qf_ = [sbt(st, "qf%d" % i, [128, 512]) for i in range(NQB)]; qsq_ = [sbt(st, "qsq%d" % i, [128, 512]) for i in range(NQB)]
                    qn_ = [sbt(st, "qn%d" % i, [128, 512]) for i in range(NQB)]; qr_ = [sbt(st, "qr%d" % i, [128, 512], BF16) for i in range(NQB)]
                    rt_ = [[sbt(st, "rt%d_%d" % (b_, i), [128, 8, 32]) for i in range(4)] for b_ in range(NQB)]
                    ssq_ = [sbt(st, "ssq%d" % i, [128, 8]) for i in range(NQB)]
                    bq_ = [Buf("qwork%d" % i) for i in range(NQB)]
                    qcnt = [0]
                    qT = [sbt(st, "qT%d" % i, [128, 8, 128], BF16) for i in range(2)]; bqT = [Buf("qT0"), Buf("qT1")]
                    S.dve(lambda e: e.memset(kT[:], 0.0), (), [bkT], cost=5.0)
                    MSET(qT[0][:], 0.0, [bqT[0]]); MSET(qT[1][:], 0.0, [bqT[1]])
                    sg = [sbt(st, "sg%d" % i, [128, 512]) for i in range(2)]; bsg = [Buf("sg0"), Buf("sg1")]
                    es = sbt(st, "es", [128, 8]); den = sbt(st, "den", [128, 4]); ya = sbt(st, "ya", [128, 256])
                    sgtmp = sbt(st, "sgtmp", [128, 512]); bsgtmp = Buf("sgtmp")
                    oT = [sbt(st, "oT%d" % i, [68, 512]) for i in range(2)]; boT = [Buf("oT0"), Buf("oT1")]
                    yga = [sbt(st, "yga%d" % i, [128, 512], BF16) for i in range(2)]; byga = [Buf("yga0"), Buf("yga1")]
                    bes = Buf("es"); bden = Buf("den")
                    MSET(vaug[:, :, :, 64:68], 1.0, [bV])
                    if is_win:
                        ACT(es[:], rowp[:, R_SINK:R_SINK + 8], AF.Exp, [bRow], [bes])

                    def qk_prep(psrc, nh, ci, is_q, pb):
                        w = nh * 64
                        bi = qcnt[0] % NQB
                        qcnt[0] += 1
                        qf, qsq, qn, qr, rt, ssq, bq = qf_[bi], qsq_[bi], qn_[bi], qr_[bi], rt_[bi], ssq_[bi], bq_[bi]
                        v3 = lambda t: t[:, 0:w].rearrange("p (h d) -> p h d", d=64)
                        ACP(qf[:, 0:w], psrc, [pb], [bq])
                        cur = qf
                        if not is_win:
                            TT(qsq[:, 0:w], qf[:, 0:w], qf[:, 0:w], ALU.mult, [bq], [bq])
                            RED(ssq[:, 0:nh], v3(qsq), ALU.add, [bq], [bq])
                            rsqrt_to(ssq[:, 0:nh], ssq[:, 0:nh], 1.0 / 64, [bq, bC], [bq])
                            if is_q:
                                TS(ssq[:, 0:nh], ssq[:, 0:nh], 0.125, None, ALU.mult, None, [bq], [bq])
                            TT(v3(qn), v3(qf), ssq[:, 0:nh].unsqueeze(2).broadcast_to([128, nh, 64]), ALU.mult, [bq], [bq])
                            r0 = R_QN if is_q else R_KN
                            TT(v3(qn), v3(qn), rowp[:, r0:r0 + 64].unsqueeze(1).broadcast_to([128, nh, 64]), ALU.mult, [bq, bRow], [bq])
                            cur = qn
                        elif is_q:
                            TS(qn[:, 0:w], qf[:, 0:w], 0.125, None, ALU.mult, None, [bq], [bq])
                            cur = qn
                        if ci >= 2:
                            j = ci - 2
                            cb = ropec[:, j, :].unsqueeze(1).broadcast_to([128, nh, 32])
                            sb_ = ropes[:, j, :].unsqueeze(1).broadcast_to([128, nh, 32])
                            x1 = v3(cur)[:, :, 0:32]; x2 = v3(cur)[:, :, 32:64]
                            t = [r[:, 0:nh, :] for r in rt]
                            TT(t[0], x1, cb, ALU.mult, [bq, bRp], [bq]); TT(t[1], x2, sb_, ALU.mult, [bq, bRp], [bq])
                            TT(v3(qr)[:, :, 0:32], t[0], t[1], ALU.subtract, [bq], [bq])
                            TT(t[2], x1, sb_, ALU.mult, [bq, bRp], [bq]); TT(t[3], x2, cb, ALU.mult, [bq, bRp], [bq])
                            TT(v3(qr)[:, :, 32:64], t[2], t[3], ALU.add, [bq], [bq])
                        else:
                            CP(qr[:, 0:w], cur[:, 0:w], [bq], [bq])
                        return qr, bq

                    for ci in range(NCH):
                        bank = ci % 2
                        for kc in range(8):
                            MM(ps[bank][:, 0:256], inT[:, kc, col(ci):col(ci) + 128], W[:, kc, 512:768], kc == 0, kc == 7,
                               [bInT[ci]] + bW(512, 768), [PB[bank]])
                        CP(vaug[:, ci, :, 0:64], ps[bank][:, 128:256].rearrange("p (g d) -> p g d", d=64), [PB[bank]], [bV])
                        qr, bq = qk_prep(ps[bank][:, 0:128], 2, ci, False, PB[bank])
                        tb_ = 2 + bank
                        for g in range(2):
                            TR(ps_bf(tb_)[0:64, g * 128:(g + 1) * 128], qr[:, g * 64:(g + 1) * 64], identb[:], [bq, bC], [PB[tb_]])
                        CP(kT[0:64, :, ci * 128:(ci + 1) * 128], ps_bf(tb_)[0:64, 0:256].rearrange("p (g t) -> p g t", g=2), [PB[tb_]], [bkT])
                    if stop == "A1":
                        S.emit()
                        if chk("A1"):
                            return
                    for n_i, ci in enumerate(chunks_out):
                        i2 = n_i % 2
                        for kc in range(8):
                            MM(ps[0][:], inT[:, kc, col(ci):col(ci) + 128], W[:, kc, 0:512], kc == 0, kc == 7, [bInT[ci]] + bW(0, 512), [PB[0]])
                        qr, bq = qk_prep(ps[0][:], 8, ci, True, PB[0])
                        for h in range(8):
                            TR(ps_bf(1)[0:64, h * 128:(h + 1) * 128], qr[:, h * 64:(h + 1) * 64], identb[:], [bq, bC], [PB[1]])
                        CP(qT[i2][0:64].rearrange("p h t -> p (h t)"), ps_bf(1)[0:64, :], [PB[1]], [bqT[i2]])
                        for kc in range(8):
                            MM(ps[0][:], inT[:, kc, col(ci):col(ci) + 128], W[:, kc, 768:1280], kc == 0, kc == 7, [bInT[ci]] + bW(768, 1280), [PB[0]])
                        sig_exp(sg[i2][:], ps[0][:], PB[0], sgtmp[:], bsgtmp, [bsg[i2]], True)
                        if ci < 2:
                            keys = [(0, None), (1, None)]
                        elif not is_win:
                            keys = [(k, None) for k in range(NCH)]
                        else:
                            keys = [(0, None), (1, None)]
                            if ci - 1 >= 2:
                                keys.append((ci - 1, triBb))
                            keys.append((ci, None))
                            if ci + 1 < NCH:
                                keys.append((ci + 1, triFb))
                        for g in range(2):
                            pt = PT[g]; bpt = bPT[g]
                            for p0_ in range(0, len(keys), 2):
                                pair = keys[p0_:p0_ + 2]
                                wsel = STW[(p0_ // 2) % len(STW)]
                                bks = [2 * wsel, 2 * wsel + 1]
                                for j_, (kc_i, msk) in enumerate(pair):
                                    MM(ps[bks[j_]][:], kT[:, g, kc_i * 128:(kc_i + 1) * 128], qT[i2][:, 4 * g:4 * g + 4, :].rearrange("p h t -> p (h t)"),
                                       True, True, [bkT, bqT[i2]], [PB[bks[j_]]])
                                n_ = len(pair)
                                ACT(pt[:, p0_:p0_ + n_, :].rearrange("p k t -> p (k t)"), psw[wsel][:, 0:512 * n_], AF.Exp,
                                    [PB[b_] for b_ in bks[:n_]], [bpt])
                                for j_, (kc_i, msk) in enumerate(pair):
                                    ki = p0_ + j_
                                    if msk is not None:
                                        TT(pt[:, ki, :].rearrange("p (h t) -> p h t", h=4), pt[:, ki, :].rearrange("p (h t) -> p h t", h=4),
                                           msk[:].unsqueeze(1).broadcast_to([128, 4, 128]), ALU.mult, [bpt, bC], [bpt])
                            ob = BK_PVT
                            for ki, (kc_i, msk) in enumerate(keys):
                                MM(ps[ob][0:68, :], vaug[:, kc_i, g, :], pt[:, ki, :], ki == 0, ki == len(keys) - 1, [bpt, bV], [PB[ob]])
                            ACP(oT[g][:], ps[ob][0:68, :], [PB[ob]], [boT[g]])
                            for h in range(4):
                                TR(ps[BK_RES][:, h * 80:h * 80 + 68], oT[g][:, h * 128:(h + 1) * 128], ident[0:68, 0:68], [boT[g], bC], [PB[BK_RES]])
                            ob = BK_RES
                            o3 = ps[ob][:, 0:320].rearrange("p (h d) -> p h d", d=80)
                            if is_win:
                                TT(den[:], o3[:, :, 64], es[:, 4 * g:4 * g + 4], ALU.add, [PB[ob], bes], [bden])
                            else:
                                CP(den[:], o3[:, :, 64], [PB[ob]], [bden])
                            RCP(den[:], den[:], [bden], [bden])
                            TT(ya[:].rearrange("p (h d) -> p h d", d=64), o3[:, :, 0:64], den[:].unsqueeze(2).broadcast_to([128, 4, 64]),
                               ALU.mult, [PB[ob], bden], [bden])
                            TT(yga[i2][:, g * 256:(g + 1) * 256], ya[:], sg[i2][:, g * 256:(g + 1) * 256], ALU.mult,
                               [bden, bsg[i2]], [byga[i2]])
                        DMA("sp", yg_d[br, ci * 128:(ci + 1) * 128, :], yga[i2][:], [byga[i2]], [bYG[br][ci]])
                    if is_win:
                        phase_B(st, [6, 7, 6, 7])
                    S.emit()
                    if chk("AC"[br]):
                        return


            with contextlib.ExitStack() as stD:
              with contextlib.ExitStack() as st:
                qTm = sbt(stD, "qTm", [128, 4, NTOK], BF16); kTm = sbt(stD, "kTm", [128, 4, NTOK], BF16)
                ktm = sbt(stD, "ktm", [128, NCH, 512], BF16); vau = sbt(stD, "vau", [128, NCH, 4, 132], BF16)
                lf = sbt(stD, "lf", [128, NCH, 8]); li = sbt(stD, "li", [128, NCH, 8]); gif = sbt(stD, "gif", [128, NCH, 16])
                bQ, bK, bKt, bVa, bG = Buf("qTm"), Buf("kTm"), Buf("ktm"), Buf("vau"), Buf("gates")
                MSET(vau[:, :, :, 128:132], 1.0, [bVa])
                with contextlib.ExitStack() as st1:
                    groups = [(0, 1, 256)] + [(2 + 4 * i, 259 + 512 * i, 512) for i in range(4)]
                    cwT = sbt(st1, "cwT", [128, 12, 3]); bcw = Buf("cwT")
                    DMA("sp", cwT[:], convT_d[l], (), [bcw])
                    NPF = 3
                    Pf = [sbt(st1, "Pf%d" % i, [128, TW]) for i in range(NPF)]; bPf = [Buf("Pf%d" % i) for i in range(NPF)]
                    zcs = [sbt(st1, "zc%d" % i, [128, TW]) for i in range(2)]; bzs = [Buf("zc0"), Buf("zc1")]
                    vT = sbt(st1, "vT", [128, NTOK], BF16); bvT = Buf("vT")
                    for i in range(NPF):
                        S.dve(lambda e, t=Pf[i]: e.memset(t[:], 0.0), (), [bPf[i]], cost=2.5)
                    cnt = 0
                    Wg, bWg = load_w(st1, "Wif", l, 5632, 16)
                    Wrs = [load_w(st1, "Wr%d" % si_, l, 4096 + 512 * si_, 512, step=128) for si_ in range(3)]
                    for si, sname in enumerate(("q", "k", "v")):
                        if True:
                            Wr, bWr = Wrs[si]
                            for h in range(4):
                                pf = Pf[(si * 4 + h) % NPF]; bpf = bPf[(si * 4 + h) % NPF]
                                zc = zcs[(si * 4 + h) % 2]; bz = bzs[(si * 4 + h) % 2]
                                for (c_first, c_col, nt) in groups:
                                    bank = cnt % 2; cnt += 1
                                    rb = [bInT[c] for c in range(c_first, c_first + nt // 128)]
                                    for kc in range(8):
                                        MM(ps[bank][:, 0:nt], Wr[:, kc, h * 128:(h + 1) * 128], inT[:, kc, c_col:c_col + nt],
                                           kc == 0, kc == 7, rb + bWr(h * 128, (h + 1) * 128), [PB[bank]])
                                    ACP(pf[:, c_col:c_col + nt], ps[bank][:, 0:nt], [PB[bank]], [bpf])
                                cidx = si * 4 + h
                                zz = zc[:, 1:2307]
                                TS(zz, pf[:, 0:2306], cwT[:, cidx, 0:1], None, ALU.mult, None, [bpf, bcw], [bz])
                                STT(zz, pf[:, 1:2307], cwT[:, cidx, 1:2], zz, ALU.mult, ALU.add, [bpf, bcw, bz], [bz])
                                STT(zz, pf[:, 2:2308], cwT[:, cidx, 2:3], zz, ALU.mult, ALU.add, [bpf, bcw, bz], [bz])
                                if sname == "q":
                                    dst, bd = qTm[:, h, :], bQ
                                elif sname == "k":
                                    dst, bd = kTm[:, h, :], bK
                                else:
                                    dst, bd = vT[:, :], bvT
                                ACT(dst[:, 0:256], zc[:, 1:257], AF.Silu, [bz], [bd])
                                ACT(dst[:, 256:NTOK], zc[:, 259:2307], AF.Silu, [bz], [bd])
                                if sname in ("k", "v"):
                                    for c0_, nb in ((0, 8), (8, 8), (16, 2)):
                                        bank = 2 + cnt % 2; cnt += 1
                                        for j in range(nb):
                                            TR(ps_bf(bank)[:, j * 128:(j + 1) * 128], dst[:, (c0_ + j) * 128:(c0_ + j + 1) * 128], identb[:],
                                               [bd, bC], [PB[bank]])
                                        srcp = ps_bf(bank)[:, 0:nb * 128].rearrange("p (c t) -> p c t", c=nb)
                                        if sname == "k":
                                            CP(ktm[:, c0_:c0_ + nb, h * 128:(h + 1) * 128], srcp, [PB[bank]], [bKt])
                                        else:
                                            CP(vau[:, c0_:c0_ + nb, h, 0:128], srcp, [PB[bank]], [bVa])
                    with contextlib.ExitStack() as st2:
                        for ci in range(NCH):
                            bank = 4 + ci % 2
                            for kc in range(8):
                                MM(ps[bank][:, 0:16], inT[:, kc, col(ci):col(ci) + 128], Wg[:, kc, :], kc == 0, kc == 7, [bInT[ci]] + bWg(0, 16), [PB[bank]])
                            CP(gif[:, ci, :], ps[bank][:, 0:16], [PB[bank]], [bG])
                        TT(li[:], gif[:, :, 0:8], rowp[:, R_BI:R_BI + 8].unsqueeze(1).broadcast_to([128, NCH, 8]), ALU.add, [bG, bRow], [bG])
                        TT(lf[:], gif[:, :, 8:16], rowp[:, R_BF:R_BF + 8].unsqueeze(1).broadcast_to([128, NCH, 8]), ALU.add, [bG, bRow], [bG])
                        ACT(lf[:], lf[:], AF.Exp, [bG], [bG], scale=-1.0)
                        ACT(lf[:], lf[:], AF.Ln, [bG, bC], [bG], bias=onec[:, 0:1])
                        TS(lf[:], lf[:], -1.0, None, ALU.mult, None, [bG], [bG])
                        S.emit()
                        if chk("D1"):
                            return
                hsum = sbt(stD, "hsum", [128, NCH, 512]); bH = [Buf("hsum%d" % c) for c in range(NCH)]
                bball = sbt(st, "bball", [128, NCH, 16]); u_all = sbt(st, "u_all", [128, NCH, 8]); U_all = sbt(st, "U_all", [128, NCH, 8])
                bG2 = Buf("bball"); bU = Buf("U_all")
                dgu = [sbt(st, "dgu%d" % i, [128, 128]) for i in range(4)]; bdgu = [Buf("dgu%d" % i) for i in range(4)]
                for ci in range(NCH):
                    bk = ci % 2
                    MM(ps[bk][:, 0:4], triF[:], lf[:, ci, 0:4], True, True, [bC, bG], [PB[bk]])
                    MM(ps[bk][:, 4:8], triB[:], lf[:, ci, 4:8], True, True, [bC, bG], [PB[bk]])
                    MM(ps[bk][:, 8:16], ones[:], lf[:, ci, :], True, True, [bC, bG], [PB[bk]])
                    CP(bball[:, ci, :], ps[bk][:, 0:16], [PB[bk]], [bG2])
                TT(u_all[:], li[:], bball[:, :, 0:8], ALU.subtract, [bG, bG2], [bG2])
                u2 = u_all[:].rearrange("p c e -> p (c e)")
                U2 = U_all[:].rearrange("p c e -> p (c e)")
                umT = sbt(st, "umT", [72, 2]); bUm = Buf("umT")
                for hf in range(2):
                    TR(ps[2 + hf][0:72, 0:128], u2[:, hf * 72:(hf + 1) * 72], ident[:], [bG2, bC], [PB[2 + hf]])
                    RED(umT[:, hf:hf + 1], ps[2 + hf][0:72, 0:128], ALU.max, [PB[2 + hf]], [bUm])
                    TS(dgu[hf][0:72, 0:72], ident[0:72, 0:72], umT[:, hf:hf + 1], None, ALU.mult, None, [bUm, bC], [bdgu[hf]])
                    MM(ps[4 + hf][:, 0:72], ones[0:72, :], dgu[hf][0:72, 0:72], True, True, [bC, bdgu[hf]], [PB[4 + hf]])
                    CP(U2[:, hf * 72:(hf + 1) * 72], ps[4 + hf][:, 0:72], [PB[4 + hf]], [bU])
                CT = [sbt(st, "CT%d" % d, [128, 4, 132]) for d in range(2)]
                bCT = [[Buf("CT%d_%d" % (d, h)) for h in range(4)] for d in range(2)]
                mst = [sbt(st, "mst%d" % d, [128, 4]) for d in range(2)]; bm = [Buf("mst0"), Buf("mst1")]
                CTs = [[sbt(st, "CTs%d_%d" % (d, i), [128, 132], BF16) for i in range(2)] for d in range(2)]
                bCTs = [[Buf("CTs%d_%d" % (d, i)) for i in range(2)] for d in range(2)]
                kw = [[sbt(st, "kw%d_%d" % (d, i), [128, 128], BF16) for i in range(2)] for d in range(2)]
                bkw = [[Buf("kw%d_%d" % (d, i)) for i in range(2)] for d in range(2)]
                sT = [[sbt(st, "sT%d_%d" % (d, i), [128, 128], BF16) for i in range(2)] for d in range(2)]
                bsT = [[Buf("sT%d_%d" % (d, i)) for i in range(2)] for d in range(2)]
                dn = [[sbt(st, "dn%d_%d" % (d, i), [128, 1]) for i in range(2)] for d in range(2)]
                bdn = [[Buf("dn%d_%d" % (d, i)) for i in range(2)] for d in range(2)]
                Mxs = [[sbt(st, "Mx%d_%d" % (d, i), [128, 4]) for i in range(2)] for d in range(2)]
                e12s = [[sbt(st, "e12%d_%d" % (d, i), [128, 12]) for i in range(2)] for d in range(2)]
                ecs = [[sbt(st, "ec%d_%d" % (d, i), [128, 8]) for i in range(2)] for d in range(2)]
                bscs = [[Buf("scal%d_%d" % (d, i)) for i in range(2)] for d in range(2)]
                hb = [[sbt(st, "hb%d_%d" % (d, i), [128, 128]) for i in range(2)] for d in range(2)]
                bhb = [[Buf("hb%d_%d" % (d, i)) for i in range(2)] for d in range(2)]
                tmpC = [[sbt(st, "tmpC%d_%d" % (d, i), [128, 132]) for i in range(2)] for d in range(2)]
                btmpC = [[Buf("tmpC%d_%d" % (d, i)) for i in range(2)] for d in range(2)]
                CSC = 128.0 ** -0.5
                orders = [list(range(NCH)), [1, 0] + list(range(NCH - 1, 1, -1))]
                for d in range(2):
                    MSET(CT[d][:], 0.0, bCT[d]); MSET(mst[d][:], 0.0, [bm[d]])
                seen = set()
                for step in range(NCH):
                    for dr in range(2):
                        ci = orders[dr][step]
                        first = ci not in seen
                        seen.add(ci)
                        trib = triFb if dr == 0 else triBb
                        cs8 = slice(dr * 4, dr * 4 + 4)
                        tk = slice(ci * 128, (ci + 1) * 128)
                        Mx = Mxs[dr][step % 2]; e12 = e12s[dr][step % 2]; ec = ecs[dr][step % 2]; bsc_ = bscs[dr][step % 2]
                        TT(Mx[:], mst[dr][:], U_all[:, ci, cs8], ALU.max, [bm[dr], bU], [bsc_])
                        TT(e12[:, 0:4], mst[dr][:], Mx[:], ALU.subtract, [bm[dr], bsc_], [bsc_])
                        TT(e12[:, 4:8], u_all[:, ci, cs8], Mx[:], ALU.subtract, [bG2, bsc_], [bsc_])
                        STT(e12[:, 8:12], bball[:, ci, cs8], -1.0, Mx[:], ALU.mult, ALU.subtract, [bG2, bsc_], [bsc_])
                        ACT(e12[:], e12[:], AF.Exp, [bsc_], [bsc_])
                        TS(ec[:], e12[:, 0:8], CSC, None, ALU.mult, None, [bsc_], [bsc_])
                        TT(mst[dr][:], bball[:, ci, 8 + dr * 4:12 + dr * 4], Mx[:], ALU.add, [bG2, bsc_], [bm[dr]])
                        for h in range(4):
                            i2 = h % 2
                            ACT(CTs[dr][i2][:], CT[dr][:, h, :], AF.Identity, [bCT[dr][h], bsc_], [bCTs[dr][i2]], scale=ec[:, h:h + 1])
                            ACT(kw[dr][i2][:], ktm[:, ci, h * 128:(h + 1) * 128], AF.Identity, [bKt, bsc_], [bkw[dr][i2]], scale=e12[:, 4 + h:5 + h])
                            b1 = 4 * dr
                            qk = ps[b1][:, i2 * 128:(i2 + 1) * 128]
                            MM(qk, kTm[:, h, tk], qTm[:, h, tk], True, True, [bK, bQ], [PB[b1]])
                            STT(sT[dr][i2][:], qk, ec[:, 4 + h:5 + h], trib[:], ALU.mult, ALU.mult, [PB[b1], bsc_, bC], [bsT[dr][i2]])
                            b2 = 4 * dr + 1 + i2
                            MM(ps[b2][:, 0:132], sT[dr][i2][:], vau[:, ci, h, :], True, False, [bsT[dr][i2], bVa], [PB[b2]])
                            MM(ps[b2][:, 0:132], qTm[:, h, tk], CTs[dr][i2][:], False, True, [bQ, bCTs[dr][i2]], [PB[b2]])
                            dnn = dn[dr][i2]; bdnn = bdn[dr][i2]
                            ACT(dnn[:], ps[b2][:, 128:129], AF.Abs, [PB[b2]], [bdnn])
                            TT(dnn[:], dnn[:], e12[:, 8 + h:9 + h], ALU.max, [bdnn, bsc_], [bdnn])
                            RCP(dnn[:], dnn[:], [bdnn], [bdnn])
                            hs = hsum[:, ci, h * 128:(h + 1) * 128]
                            if first:
                                ACT(hs, ps[b2][:, 0:128], AF.Identity, [PB[b2], bdnn], [bH[ci]], scale=dnn[:, 0:1])
                            else:
                                STT(hs, ps[b2][:, 0:128], dnn[:, 0:1], hs, ALU.mult, ALU.add, [PB[b2], bdnn, bH[ci]], [bH[ci]])
                            b3 = 4 * dr + 3
                            cp_ = ps[b3][:, i2 * 256:i2 * 256 + 132]
                            MM(cp_, kw[dr][i2][:], vau[:, ci, h, :], True, True, [bkw[dr][i2], bVa], [PB[b3]])
                            STT(CT[dr][:, h, :], CT[dr][:, h, :], e12[:, h:h + 1], cp_, ALU.mult, ALU.add, [bCT[dr][h], bsc_, PB[b3]], [bCT[dr][h]])
                S.emit()
                if chk("D2"):
                    return
              if True:
                with contextlib.ExitStack() as st2:
                    W, bW = load_w(st2, "Wog", l, 5648, 1024)
                    so = [sbt(st2, "so%d" % i, [128, 512]) for i in range(2)]; bso = [Buf("so0"), Buf("so1")]
                    sgm = [sbt(st2, "sgm%d" % i, [128, 512]) for i in range(2)]; bsgm = [Buf("sgm0"), Buf("sgm1")]
                    hsq_ = [sbt(st2, "hsq%d" % i, [128, 512]) for i in range(2)]; hn_ = [sbt(st2, "hn%d" % i, [128, 512]) for i in range(2)]
                    s4_ = [sbt(st2, "s4%d" % i, [128, 4]) for i in range(2)]; bh_ = [Buf("hwork0"), Buf("hwork1")]
                    stmp = [sbt(st2, "stmp%d" % i, [128, 512]) for i in range(2)]; bstmp = [Buf("stmp0"), Buf("stmp1")]
                    ygm = [sbt(st2, "ygm%d" % i, [128, 512], BF16) for i in range(2)]; bygm = [Buf("ygm0"), Buf("ygm1")]
                    for n_i, ci in enumerate(chunks_out):
                        i2 = n_i % 2
                        hsq, hn, s4, bh = hsq_[i2], hn_[i2], s4_[i2], bh_[i2]
                        cs = slice(col(ci), col(ci) + 128)
                        p0 = 2 * i2
                        for kc in range(8):
                            MM(ps[p0][:], inT[:, kc, cs], W[:, kc, 0:512], kc == 0, kc == 7, [bInT[ci]] + bW(0, 512), [PB[p0]])
                        sig_exp(so[i2][:], ps[p0][:], PB[p0], stmp[0][:], bstmp[0], [bso[i2]], False)
                        for kc in range(8):
                            MM(ps[p0 + 1][:], inT[:, kc, cs], W[:, kc, 512:1024], kc == 0, kc == 7, [bInT[ci]] + bW(512, 1024), [PB[p0 + 1]])
                        sig_exp(sgm[i2][:], ps[p0 + 1][:], PB[p0 + 1], stmp[1][:], bstmp[1], [bsgm[i2]], True)
                        h3 = lambda t: t.rearrange("p (h d) -> p h d", d=128)
                        TT(hsq[:], hsum[:, ci, :], hsum[:, ci, :], ALU.mult, [bH[ci]], [bh])
                        RED(s4[:], h3(hsq[:]), ALU.add, [bh], [bh])
                        rsqrt_to(s4[:], s4[:], 1.0 / 128, [bh, bC], [bh])
                        TT(h3(hn[:]), h3(hsum[:, ci, :]), s4[:].unsqueeze(2).broadcast_to([128, 4, 128]), ALU.mult, [bH[ci], bh], [bh])
                        TT(hn[:], hn[:], rowp[:, R_MN:R_MN + 512], ALU.mult, [bh, bRow], [bh])
                        TT(hn[:], hn[:], so[i2][:], ALU.mult, [bh, bso[i2]], [bh])
                        TT(ygm[i2][:], hn[:], sgm[i2][:], ALU.mult, [bh, bsgm[i2]], [bygm[i2]])
                        DMA("sp", yg_d[3, ci * 128:(ci + 1) * 128, :], ygm[i2][:], [bygm[i2]], [bYG[3][ci]])
                    S.emit()
                    if chk("D3"):
                        return

            with contextlib.ExitStack() as st:
                sTt = sbt(st, "sTt", [128, 8, NTOK], BF16); bsTt = [Buf("sTt%d" % c) for c in range(NCH)]
                WmP = sbt(st, "WmP", [128, 8, 512], BF16); WbP = sbt(st, "WbP", [128, 4, 512], BF16); bWmP, bWbP = Buf("WmP"), Buf("WbP")
                WoP = sbt(st, "WoP", [128, 8, 512], BF16); bWoP = Buf("WoP")
                with contextlib.ExitStack() as st1:
                    for p in range(2):
                        with contextlib.ExitStack() as st2:
                            Wm = sbt(st2, "Wm", [128, 8, 4, 512], BF16); bWm = [Buf("Wm%d" % k_) for k_ in range(4)]
                            Wb = sbt(st2, "Wb", [128, 4, 4, 512], BF16); bWb = [Buf("Wb%d" % k_) for k_ in range(4)]
                            src = win_d[l].rearrange("(kc p) n -> p kc n", p=128)
                            for k in range(4):
                                if p == 1 and k == 0:
                                    bWm[0], bWb[0] = bWmP, bWbP
                                    continue
                                c0 = 6672 + k * 1024 + p * 512
                                DMA("pool", Wm[:, :, k, :], src[:, :, c0:c0 + 512], (), [bWm[k]])
                                DMA("pool", Wb[:, k, :, :], wbr_d[l, k].rearrange("(fc p) n -> p fc n", p=128)[:, :, p * 512:(p + 1) * 512], (), [bWb[k]])
                            if p == 0:
                                DMA("pool", WmP[:], src[:, :, 6672 + 512:6672 + 1024], (), [bWmP])
                                DMA("pool", WbP[:], wbr_d[l, 0].rearrange("(fc p) n -> p fc n", p=128)[:, :, 512:1024], (), [bWbP])
                            else:
                                DMA("pool", WoP[:], wout_d[l].rearrange("(kc p) n -> p kc n", p=128)[:, :, 0:512], (), [bWoP])

                            def wm_ap(kc_, k_):
                                return WmP[:, kc_, :] if (p == 1 and k_ == 0) else Wm[:, kc_, k_, :]

                            def wb_ap(k_, fc_):
                                return WbP[:, fc_, :] if (p == 1 and k_ == 0) else Wb[:, k_, fc_, :]
                            ygl = [sbt(st2, "ygl%d" % i, [128, 4, 512], BF16) for i in range(2)]; bygl = [Buf("ygl0"), Buf("ygl1")]
                            ygT = [sbt(st2, "ygT%d" % i, [128, 16, 128], BF16) for i in range(2)]; bygT = [Buf("ygT0"), Buf("ygT1")]
                            sig = [sbt(st2, "sig%d" % i, [128, 512]) for i in range(2)]; bsig = [Buf("sig0"), Buf("sig1")]
                            sacc_ = [sbt(st2, "sacc%d" % i, [128, 512]) for i in range(2)]; tmpm_ = [sbt(st2, "tmpm%d" % i, [128, 512]) for i in range(2)]
                            sbf_ = [sbt(st2, "sbf%d" % i, [128, 512], BF16) for i in range(2)]
                            bsa_ = [Buf("sacc0"), Buf("sacc1")]
                            for n_i, ci in enumerate(chunks_out):
                                i2 = n_i % 2
                                sacc, tmpm, sbf, bsa = sacc_[i2], tmpm_[i2], sbf_[i2], bsa_[i2]
                                cs = slice(col(ci), col(ci) + 128)
                                for k in range(4):
                                    DMA("sp", ygl[i2][:, k, :], yg_d[k, ci * 128:(ci + 1) * 128, :], [bYG[k][ci]], [bygl[i2]])
                                for k in range(4):
                                    for fc in range(4):
                                        idx = k * 4 + fc
                                        bank = 6 + idx // 8
                                        TR(ps_bf(bank)[:, (idx % 8) * 128:(idx % 8 + 1) * 128], ygl[i2][:, k, fc * 128:(fc + 1) * 128], identb[:],
                                           [bygl[i2], bC], [PB[bank]])
                                CP(ygT[i2][:, 0:8, :].rearrange("p a t -> p (a t)"), ps_bf(6)[:, :], [PB[6]], [bygT[i2]])
                                CP(ygT[i2][:, 8:16, :].rearrange("p a t -> p (a t)"), ps_bf(7)[:, :], [PB[7]], [bygT[i2]])
                                for k in range(4):
                                    k2 = k % 2
                                    for fc in range(4):
                                        MM(ps[k2][:], ygT[i2][:, k * 4 + fc, :], wb_ap(k, fc), fc == 0, fc == 3, [bygT[i2], bWb[k]], [PB[k2]])
                                    for kc in range(8):
                                        MM(ps[2 + k2][:], inT[:, kc, cs], wm_ap(kc, k), kc == 0, kc == 7, [bInT[ci], bWm[k]], [PB[2 + k2]])
                                    ACT(sig[k2][:], ps[2 + k2][:], AF.Sigmoid, [PB[2 + k2]], [bsig[k2]])
                                    if k == 0:
                                        TT(sacc[:], ps[k2][:], sig[k2][:], ALU.mult, [PB[k2], bsig[k2]], [bsa])
                                    else:
                                        TT(tmpm[:], ps[k2][:], sig[k2][:], ALU.mult, [PB[k2], bsig[k2]], [bsa])
                                        TT(sacc[:], sacc[:], tmpm[:], ALU.add, [bsa], [bsa])
                                CP(sbf[:], sacc[:], [bsa], [bsa])
                                for dc in range(4):
                                    TR(ps_bf(4 + i2)[:, dc * 128:(dc + 1) * 128], sbf[:, dc * 128:(dc + 1) * 128], identb[:], [bsa, bC], [PB[4 + i2]])
                                CP(sTt[:, p * 4:(p + 1) * 4, ci * 128:(ci + 1) * 128], ps_bf(4 + i2)[:, 0:512].rearrange("p (a t) -> p a t", a=4),
                                   [PB[4 + i2]], [bsTt[ci]])
                            S.emit()
                            if chk("E%d" % p):
                                return
                with contextlib.ExitStack() as st2:
                    Wo = sbt(st2, "Wo", [128, 8, D], BF16); bWo = [Buf("Wo0"), Buf("Wo1")]
                    bWo[0] = bWoP
                    for hf_ in range(1, 2):
                        DMA("pool", Wo[:, :, hf_ * 512:(hf_ + 1) * 512], wout_d[l].rearrange("(kc p) n -> p kc n", p=128)[:, :, hf_ * 512:(hf_ + 1) * 512], (), [bWo[hf_]])
                    yf = [sbt(st2, "yf%d" % i, [128, D]) for i in range(2)]; byf = [Buf("yf0"), Buf("yf1")]
                    rs = [sbt(st2, "rs%d" % i, [128, D]) for i in range(2)]; brs = [Buf("rs0"), Buf("rs1")]
                    ob_ = [sbt(st2, "ob%d" % i, [128, D]) for i in range(2)]; bob = [Buf("ob0"), Buf("ob1")]
                    junk = sbt(st2, "junkf", [128, D], BF16); bj = Buf("junkf")
                    ss = [sbt(st2, "ssf%d" % i, [128, 1]) for i in range(2)]; bss = [Buf("ssf0"), Buf("ssf1")]
                    for n_i, ci in enumerate(chunks_out):
                        i2 = n_i % 2
                        v = 1 if ci < 2 else 0
                        tk = slice(ci * 128, (ci + 1) * 128)
                        DMA("sp", rs[i2][:], src_d[tk, :], [bRes[l][ci]], [brs[i2]])
                        for half in range(2):
                            bank = 2 * i2 + half
                            for dc in range(8):
                                MM(ps[bank][:], sTt[:, dc, tk], (WoP[:, dc, :] if half == 0 else Wo[:, dc, 512:1024]), dc == 0, dc == 7, [bsTt[ci], bWo[half]], [PB[bank]])
                            ACP(yf[i2][:, half * 512:(half + 1) * 512], ps[bank][:], [PB[bank]], [byf[i2]])
                        MSET(ss[i2][:], 0.0, [bss[i2]])
                        ACT(junk[:], yf[i2][:], AF.Square, [byf[i2], bss[i2]], [bj, bss[i2]], accum=ss[i2][:])
                        rsqrt_to(ss[i2][:], ss[i2][:], 1.0 / D, [bss[i2], bC], [bss[i2]])
                        STT(yf[i2][:], yf[i2][:], ss[i2][:, 0:1], gprow[v][:], ALU.mult, ALU.mult, [byf[i2], bss[i2], bGp[v]], [byf[i2]])
                        TT(ob_[i2][:], yf[i2][:], rs[i2][:], ALU.add, [byf[i2], brs[i2]], [bob[i2]])
                        if last:
                            DMA("sp", out_d[(ci - 2) * 128:(ci - 1) * 128, :], ob_[i2][:], [bob[i2]], [bRes[l + 1][ci]])
                        else:
                            DMA("sp", r1_d[tk, :], ob_[i2][:], [bob[i2]], [bRes[l + 1][ci]])
                    S.emit()
                    if chk("F"):
                        return
        try:
            for l_ in range(depth):
                if not stopflag[0]:
                    layer(l_)
        except _Stop:
            pass
        S.finish()
        stats = dict(n_ops=len(S.ops), n_wait=S.nwait)
    return nc, stats


def _fm(v, n):
    return np.ascontiguousarray(np.asarray(v, np.float32).reshape(n, 128).T)


def make_inputs(inputs):
    f = lambda k: np.asarray(inputs[k], np.float32)
    x, c, ctx, c_ctx = f("x"), f("c"), f("ctx"), f("c_ctx")
    L = 2
    shared = {}
    shared["w_mod"] = np.ascontiguousarray(f("w_mod"))
    shared["b_modT"] = np.stack([_fm(f("b_mod")[l], 24) for l in range(L)])
    shared["g_preT"] = np.stack([_fm(f("g_pre")[l], 8) for l in range(L)])
    shared["g_postT"] = np.stack([_fm(f("g_post")[l], 8) for l in range(L)])
    shared["w_in"] = np.ascontiguousarray(f("w_in"))
    rows = []
    for l in range(L):
        r = np.concatenate([f("a_q_norm")[l], f("a_k_norm")[l], f("w_sink")[l], f("sg_ln_g")[l], f("sg_ln_b")[l], f("m_norm")[l],
                            f("m_b_i")[l].reshape(-1), f("m_b_f")[l].reshape(-1)])
        assert r.shape[0] == NROW
        rows.append(np.broadcast_to(r[None, :], (128, NROW)))
    shared["rowp"] = np.ascontiguousarray(np.stack(rows))
    shared["convT"] = np.ascontiguousarray(f("m_conv").reshape(L, 3, 12, 128).transpose(0, 3, 2, 1))
    shared["sg_wT"] = np.ascontiguousarray(f("sg_w").transpose(0, 3, 1, 2))
    shared["sg_bT"] = np.ascontiguousarray(f("sg_b").transpose(0, 2, 1))
    shared["w_branch"] = np.ascontiguousarray(f("w_branch"))
    shared["w_out"] = np.ascontiguousarray(f("w_out"))
    shared["ident"] = np.eye(128, dtype=np.float32)
    s_ = np.arange(128)
    shared["triF"] = (s_[:, None] <= s_[None, :]).astype(np.float32)
    shared["triB"] = (s_[:, None] >= s_[None, :]).astype(np.float32)
    t = np.arange(2048)
    row = (t // 64).astype(np.float32); cl = (t % 64).astype(np.float32)
    inv = (np.float32(10000.0) ** (-np.arange(16, dtype=np.float32) / np.float32(16))).astype(np.float32)
    ang = np.concatenate([row[:, None] * inv, cl[:, None] * inv], axis=-1).astype(np.float32)
    shared["ropec"] = np.ascontiguousarray(np.cos(ang).astype(np.float32).reshape(16, 128, 32).transpose(1, 0, 2))
    shared["ropes"] = np.ascontiguousarray(np.sin(ang).astype(np.float32).reshape(16, 128, 32).transpose(1, 0, 2))
    in_maps = []
    for b in range(x.shape[0]):
        m = dict(shared)
        m["xin"] = np.ascontiguousarray(np.concatenate([ctx[b], x[b]], axis=0))
        m["cc"] = np.ascontiguousarray(np.stack([_fm(c[b], 8), _fm(c_ctx, 8)], axis=-1))
        in_maps.append(m)
    return in_maps


_CACHE = {}


def kernel(**inputs):
    in_maps = make_inputs(inputs)
    if "nc" not in _CACHE:
        _CACHE["nc"] = build()[0]
    nc = _CACHE["nc"]
    res = run_bass_kernel_spmd(nc, in_maps, core_ids=list(range(len(in_maps))))
    out = np.stack([np.asarray(r["out"], np.float32) for r in res.results], axis=0)
    return out
```
